# Optimizing a Trainium2 kernel written in Bass

```python
import jax, jax.numpy as jnp
from jax import lax
import numpy as np

D_MODEL = 2048
BATCH = 4
SEQ = 2048
DEPTH = 2

N_MIXERS = 2
N_LAYERS_A = (DEPTH + N_MIXERS - 1) // N_MIXERS
N_LAYERS_B = DEPTH // N_MIXERS
CHUNK = 64
EPS = 1e-6

A_HEADS = 8
A_DV = D_MODEL // A_HEADS
A_DK = A_DV // 2
A_QK = A_HEADS * A_DK
A_V = A_HEADS * A_DV
A_IN = 2 * A_QK + 2 * A_V + 4 * A_HEADS

B_EXPAND = 128
B_HEADS = D_MODEL // B_EXPAND
B_DK = B_EXPAND
B_DV = D_MODEL // B_HEADS
B_F = B_HEADS * B_DK
B_V = B_HEADS * B_DV
B_IN = 3 * B_F + 2 * B_V

D_FF = 5504
CONV_W = 3

kernel_name = "hybrid_mlstm_hgrn2_convglu_encoder"


def rmsnorm(x, g):
    xf = x.astype(jnp.float32)
    xf = xf * lax.rsqrt(jnp.mean(xf * xf, axis=-1, keepdims=True) + EPS)
    return xf.astype(x.dtype) * g


def head_rmsnorm(y, g, n_heads):
    b, s, w = y.shape
    yf = y.astype(jnp.float32).reshape(b, s, n_heads, w // n_heads)
    yf = yf * lax.rsqrt(jnp.mean(yf * yf, axis=-1, keepdims=True) + EPS)
    return yf.reshape(b, s, w).astype(y.dtype) * g


def to_chunks(t):
    b, s = t.shape[0], t.shape[1]
    t = t.reshape((b, s // CHUNK, CHUNK) + t.shape[2:])
    perm = (1, 0, 3, 2) + tuple(range(4, t.ndim))
    return t.transpose(perm)


def from_chunks(t):
    nc, b, h, l, d = t.shape
    return t.transpose(1, 0, 3, 2, 4).reshape(b, nc * l, h, d)


def _mlstm_chunk(carry, xs):
    c_st, n_st, m_st = carry
    q, k, v, ig, lf = xs
    L = q.shape[2]
    b = jnp.cumsum(lf, axis=-1)
    g_tot = b[..., -1]
    lower = jnp.tril(jnp.ones((L, L), dtype=bool))
    d = jnp.where(lower, b[..., :, None] - b[..., None, :] + ig[..., None, :], -jnp.inf)
    m_inter = b + m_st[..., None]
    m_t = jnp.maximum(jnp.max(d, axis=-1), m_inter)
    s = jnp.einsum('bhtk,bhsk->bhts', q, k) * jnp.exp(d - m_t[..., None])
    w_inter = jnp.exp(m_inter - m_t)
    num = jnp.einsum('bhts,bhsv->bhtv', s, v) + w_inter[..., None] * jnp.einsum('bhtk,bhvk->bhtv', q, c_st)
    den = jnp.sum(s, axis=-1) + w_inter * jnp.einsum('bhtk,bhk->bht', q, n_st)
    h = num / jnp.maximum(jnp.abs(den), jnp.exp(-m_t))[..., None]
    a = g_tot[..., None] - b + ig
    m_new = jnp.maximum(g_tot + m_st, jnp.max(a, axis=-1))
    w_s = jnp.exp(a - m_new[..., None])
    decay = jnp.exp(g_tot + m_st - m_new)
    c_st = decay[..., None, None] * c_st + jnp.einsum('bhs,bhsv,bhsk->bhvk', w_s, v, k)
    n_st = decay[..., None] * n_st + jnp.einsum('bhs,bhsk->bhk', w_s, k)
    return (c_st, n_st, m_new), h


def mlstm_direction(q, k, v, ig, lf):
    b, _, h, dk = q.shape
    dv = v.shape[-1]
    init = (jnp.zeros((b, h, dv, dk), jnp.float32), jnp.zeros((b, h, dk), jnp.float32),
            jnp.zeros((b, h), jnp.float32))
    xs = (to_chunks(q), to_chunks(k), to_chunks(v), to_chunks(ig), to_chunks(lf))
    _, out = lax.scan(_mlstm_chunk, init, xs)
    return from_chunks(out)


def mlstm_mixer(hn, w_in, b_gate, head_g, w_out):
    b, s, _ = hn.shape
    p = hn @ w_in
    q, k, v, o, gates = jnp.split(p, [A_QK, 2 * A_QK, 2 * A_QK + A_V, 2 * A_QK + 2 * A_V], axis=-1)
    gates = (gates + b_gate).astype(jnp.float32).reshape(b, s, 4, A_HEADS)
    ig_f, lf_f = gates[:, :, 0], jax.nn.log_sigmoid(gates[:, :, 1])
    ig_b, lf_b = gates[:, :, 2], jax.nn.log_sigmoid(gates[:, :, 3])
    q = q.astype(jnp.float32).reshape(b, s, A_HEADS, A_DK) * (A_DK ** -0.5)
    k = k.astype(jnp.float32).reshape(b, s, A_HEADS, A_DK)
    v = v.astype(jnp.float32).reshape(b, s, A_HEADS, A_DV)
    fl = lambda t: jnp.flip(t, axis=1)
    y = mlstm_direction(q, k, v, ig_f, lf_f) + fl(mlstm_direction(fl(q), fl(k), fl(v), fl(ig_b), fl(lf_b)))
    y = y.reshape(b, s, A_V).astype(hn.dtype)
    y = head_rmsnorm(y, head_g, A_HEADS) * jax.nn.sigmoid(o)
    return y @ w_out


def _hgrn_chunk(s_st, xs):
    q, k, lf, v = xs
    L = q.shape[2]
    b = jnp.cumsum(lf, axis=2)
    lower = jnp.tril(jnp.ones((L, L), dtype=bool))
    rel = jnp.where(lower[:, :, None], b[:, :, :, None, :] - b[:, :, None, :, :], -jnp.inf)
    attn = jnp.einsum('bhtsk,bhsk->bhts', jnp.exp(rel) * q[:, :, :, None, :], k)
    o = jnp.einsum('bhts,bhsv->bhtv', attn, v) + jnp.einsum('bhtk,bhkv->bhtv', q * jnp.exp(b), s_st)
    b_end = b[:, :, -1]
    s_st = jnp.exp(b_end)[..., None] * s_st + jnp.einsum('bhsk,bhsv->bhkv', k * jnp.exp(b_end[:, :, None] - b), v)
    return s_st, o


def hgrn_direction(q, k, lf, v):
    b, _, h, dk = q.shape
    dv = v.shape[-1]
    init = jnp.zeros((b, h, dk, dv), jnp.float32)
    _, out = lax.scan(_hgrn_chunk, init, (to_chunks(q), to_chunks(k), to_chunks(lf), to_chunks(v)))
    return from_chunks(out)


def hgrn_mixer(hn, w_in, lb, head_g, w_out):
    b, s, _ = hn.shape
    p = hn @ w_in
    q, i, g, f_f, f_b = jnp.split(p, [B_F, B_F + B_V, B_F + 2 * B_V, 2 * B_F + 2 * B_V], axis=-1)
    log_lb, log_1mlb = jnp.log(lb), jnp.log1p(-lb)

    def gate(f):
        f = f.astype(jnp.float32)
        log_f = jnp.logaddexp(log_lb, log_1mlb + jax.nn.log_sigmoid(f))
        k = (1.0 - lb) * jax.nn.sigmoid(-f)
        return (k.reshape(b, s, B_HEADS, B_DK), log_f.reshape(b, s, B_HEADS, B_DK))

    k_f, lf_f = gate(f_f)
    k_b, lf_b = gate(f_b)
    q = q.astype(jnp.float32).reshape(b, s, B_HEADS, B_DK)
    v = i.astype(jnp.float32).reshape(b, s, B_HEADS, B_DV)
    fl = lambda t: jnp.flip(t, axis=1)
    y = hgrn_direction(q, k_f, lf_f, v) + fl(hgrn_direction(fl(q), fl(k_b), fl(lf_b), fl(v)))
    y = y.reshape(b, s, B_V).astype(hn.dtype)
    y = head_rmsnorm(y, head_g, B_HEADS) * jax.nn.silu(g)
    return y @ w_out


def conv_glu(hn, w_up, conv_w, conv_b, w_down):
    u = hn @ w_up
    a, v = jnp.split(u, 2, axis=-1)
    s = a.shape[1]
    pad = CONV_W // 2
    ap = jnp.pad(a, ((0, 0), (pad, pad), (0, 0)))
    c = conv_b
    for j in range(CONV_W):
        c = c + ap[:, j:j + s] * conv_w[j]
    return (jax.nn.gelu(c, approximate=False) * v) @ w_down


def setup_inputs(seed: int = 0) -> dict:
    key = jax.random.key(seed)
    ks = jax.random.split(key, 20)
    f32 = jnp.float32
    nrm = lambda k, shape, scale: jax.random.normal(k, shape, f32) * scale
    i_bias = -3.0 + 0.1 * jax.random.normal(ks[0], (N_LAYERS_A, 2, A_HEADS), f32)
    f_bias = jnp.linspace(3.0, 6.0, A_HEADS, dtype=f32) + 0.1 * jax.random.normal(ks[1], (N_LAYERS_A, 2, A_HEADS), f32)
    b_gate = jnp.stack([i_bias[:, 0], f_bias[:, 0], i_bias[:, 1], f_bias[:, 1]], axis=1).reshape(N_LAYERS_A, 4 * A_HEADS)
    return {
        "x": jax.random.normal(ks[2], (BATCH, SEQ, D_MODEL), f32),
        "norm_mix_g": 1.0 + nrm(ks[3], (DEPTH, D_MODEL), 0.02),
        "norm_ffn_g": 1.0 + nrm(ks[4], (DEPTH, D_MODEL), 0.02),
        "mlstm_w_in": nrm(ks[5], (N_LAYERS_A, D_MODEL, A_IN), D_MODEL ** -0.5),
        "mlstm_b_gate": b_gate,
        "mlstm_head_g": 1.0 + nrm(ks[6], (N_LAYERS_A, A_V), 0.02),
        "mlstm_w_out": nrm(ks[7], (N_LAYERS_A, A_V, D_MODEL), A_V ** -0.5),
        "hgrn_w_in": nrm(ks[8], (N_LAYERS_B, D_MODEL, B_IN), D_MODEL ** -0.5),
        "hgrn_lb": nrm(ks[9], (DEPTH, B_F), 0.5),
        "hgrn_head_g": 1.0 + nrm(ks[10], (N_LAYERS_B, B_V), 0.02),
        "hgrn_w_out": nrm(ks[11], (N_LAYERS_B, B_V, D_MODEL), B_V ** -0.5),
        "ffn_w_up": nrm(ks[12], (DEPTH, D_MODEL, 2 * D_FF), D_MODEL ** -0.5),
        "ffn_conv_w": nrm(ks[13], (DEPTH, CONV_W, D_FF), CONV_W ** -0.5),
        "ffn_conv_b": nrm(ks[14], (DEPTH, D_FF), 0.02),
        "ffn_w_down": nrm(ks[15], (DEPTH, D_FF, D_MODEL), D_FF ** -0.5),
        "final_g": 1.0 + nrm(ks[16], (D_MODEL,), 0.02),
    }


def reference(x, norm_mix_g, norm_ffn_g, mlstm_w_in, mlstm_b_gate, mlstm_head_g, mlstm_w_out,
              hgrn_w_in, hgrn_lb, hgrn_head_g, hgrn_w_out, ffn_w_up, ffn_conv_w, ffn_conv_b,
              ffn_w_down, final_g):
    sm = jax.nn.softmax(hgrn_lb.astype(jnp.float32), axis=0)
    lower_bounds = jnp.cumsum(sm, axis=0) - sm[0]
    h = x
    for layer in range(DEPTH):
        j = layer // N_MIXERS
        hn = rmsnorm(h, norm_mix_g[layer])
        if layer % N_MIXERS == 0:
            mix = mlstm_mixer(hn, mlstm_w_in[j], mlstm_b_gate[j], mlstm_head_g[j], mlstm_w_out[j])
        else:
            mix = hgrn_mixer(hn, hgrn_w_in[j], lower_bounds[layer], hgrn_head_g[j], hgrn_w_out[j])
        h = h + mix
        h = h + conv_glu(rmsnorm(h, norm_ffn_g[layer]), ffn_w_up[layer], ffn_conv_w[layer],
                         ffn_conv_b[layer], ffn_w_down[layer])
    return rmsnorm(h, final_g)
```

```python
import numpy as np
from contextlib import ExitStack
import concourse.bass as bass
import concourse.mybir as mybir
from concourse.bass_utils import run_bass_kernel_spmd

F32 = mybir.dt.float32
BF16 = mybir.dt.bfloat16
AF = mybir.ActivationFunctionType
ALU = mybir.AluOpType
AX = mybir.AxisListType

T = 1024
NT = 8
D = 2048
KC = 16
DFF = 5504
NFF = 43
EPS = 1e-6
A_H = 8
B_H = 16
WB = 512


class Buf:
    __slots__ = ("name", "w", "r")

    def __init__(self, name=""):
        self.name = name
        self.w = None
        self.r = {}


class Sched:
    ENGS = ("pe", "act", "dve", "pool", "sp")

    def __init__(self, nc, stack, n_dma_sems=12, same_engine_sync=True):
        self.nc = nc
        self.same_engine_sync = same_engine_sync
        self.thunks = {e: [] for e in self.ENGS}
        self.sems = {}
        self.count = {}
        self.waited = {e: {} for e in self.ENGS}
        for e in self.ENGS:
            self.sems[e] = stack.enter_context(nc.semaphore("s_" + e))
            self.count[e] = 0
        self.dma_sems = {}
        self.dma_rr = {}
        for q in ("sp", "pool", "act"):
            lst = []
            for i in range(n_dma_sems):
                key = "d_%s_%d" % (q, i)
                self.sems[key] = stack.enter_context(nc.semaphore(key))
                self.count[key] = 0
                lst.append(key)
            self.dma_sems[q] = lst
            self.dma_rr[q] = 0
        self.n_inst = 0
        self.bufs = {}
        self.disabled = False
        self.want_pid = False
        self.pid = None

    def B(self, *key):
        b = self.bufs.get(key)
        if b is None:
            b = Buf(str(key))
            self.bufs[key] = b
        return b

    def _deps(self, eng, reads, writes):
        deps = {}

        def add(k, v):
            if v > deps.get(k, 0):
                deps[k] = v
        for b in reads:
            if b.w is not None:
                add(*b.w)
        for b in writes:
            if b.w is not None:
                add(*b.w)
            for k, v in b.r.items():
                add(k, v)
        out = []
        for k, v in deps.items():
            if k == eng and (eng == "pe" or not self.same_engine_sync):
                continue
            if self.waited[eng].get(k, 0) >= v:
                continue
            self.waited[eng][k] = v
            out.append((k, v))
        return out

    def _emit_waits(self, eng, waits):
        sems = self.sems
        for k, v in waits:
            self.thunks[eng].append(lambda e, k=k, v=v: e.wait_ge(sems[k], v))

    def op(self, eng, fn, reads=(), writes=(), inc=True):
        if self.disabled:
            return
        waits = self._deps(eng, reads, writes)
        self._emit_waits(eng, waits)
        val = self.count[eng] + 1
        if inc:
            self.count[eng] = val
            sem = self.sems[eng]
            self.thunks[eng].append(lambda e: fn(e).then_inc(sem, 1))
        else:
            self.thunks[eng].append(lambda e: fn(e))
        for b in reads:
            if b.r.get(eng, 0) < val:
                b.r[eng] = val
        for b in writes:
            b.w = (eng, val)
            b.r = {}
        self.n_inst += 1

    def dma(self, q, out, in_, reads=(), writes=(), fn=None, **kw):
        if self.disabled:
            return
        i = self.dma_rr[q]
        self.dma_rr[q] = (i + 1) % len(self.dma_sems[q])
        key = self.dma_sems[q][i]
        waits = self._deps(q, reads, writes)
        prev = self.count[key]
        if prev > 0 and self.waited[q].get(key, 0) < prev:
            self.waited[q][key] = prev
            waits.append((key, prev))
        self._emit_waits(q, waits)
        val = prev + 16
        self.count[key] = val
        sem = self.sems[key]
        if fn is None:
            self.thunks[q].append(
                lambda e: e.dma_start(out=out, in_=in_, **kw).then_inc(sem, 16))
        else:
            self.thunks[q].append(lambda e: fn(e).then_inc(sem, 16))
        for b in reads:
            if b.r.get(key, 0) < val:
                b.r[key] = val
        for b in writes:
            b.w = (key, val)
            b.r = {}
        self.n_inst += 1

    def barrier(self):
        if self.disabled:
            return
        keys = list(self.count.keys())
        for e in self.ENGS:
            waits = []
            for k in keys:
                v = self.count[k]
                if k == e or v == 0:
                    continue
                if self.waited[e].get(k, 0) >= v:
                    continue
                self.waited[e][k] = v
                waits.append((k, v))
            self._emit_waits(e, waits)

    def finish(self):
        self.barrier()
        nc = self.nc
        th = self.thunks
        with nc.Block() as block:
            @block.tensor
            def _(e):
                for t in th["pe"]:
                    t(e)

            @block.scalar
            def _(e):
                for t in th["act"]:
                    t(e)

            @block.vector
            def _(e):
                for t in th["dve"]:
                    t(e)

            @block.gpsimd
            def _(e):
                for t in th["pool"]:
                    t(e)

            @block.sync
            def _(e):
                if self.want_pid:
                    self.pid = e.partition_id()
                for t in th["sp"]:
                    t(e)


class StopBuild(Exception):
    pass


class Prog:
    def __init__(self, n_stages=5, dbg=None, stop=None, n_cores=8):
        self.n_cores = n_cores
        self.stop = stop
        self.v = 0
        self.n_stages = n_stages
        self.dbg = dbg
        self.nc = bass.Bass("TRN2", target_bir_lowering=False, num_devices=self.n_cores)
        self.out_names = []

    def din(self, name, shape, dt=F32):
        return self.nc.dram_tensor(name, list(shape), dt, kind="ExternalInput").ap()

    def dout(self, name, shape, dt=F32):
        self.out_names.append(name)
        return self.nc.dram_tensor(name, list(shape), dt, kind="ExternalOutput").ap()

    def dscr(self, name, shape, dt):
        return self.nc.dram_tensor(name, list(shape), dt, kind="Internal").ap()

    def sb(self, st, name, shape, dt):
        self.uid = getattr(self, "uid", 0) + 1
        return st.enter_context(self.nc.sbuf_tensor("%s_u%d" % (name, self.uid), list(shape), dt))

    def bank(self, j):
        return self.ps[j // 2][:, (j % 2) * 512:(j % 2 + 1) * 512]

    def PB(self, j):
        return self.S.B("psum", j)

    def chk(self, tag):
        if self.stop == tag:
            self.S.barrier()
            self.S.disabled = True

    def build(self):
        nc = self.nc
        with ExitStack() as st:
            self.st = st
            S = self.S = Sched(nc, st)
            self.declare_io()
            self.h = self.sb(st, "h", [128, NT, D], F32)
            self.gbc = self.sb(st, "gbc", [128, D], F32)
            self.ident = self.sb(st, "ident", [128, 128], BF16)
            self.identf = self.sb(st, "identf", [128, 128], F32)
            self.m_le = self.sb(st, "m_le", [128, 128], F32)
            self.m_ge = self.sb(st, "m_ge", [128, 128], F32)
            self.ones = self.sb(st, "ones", [128, 128], F32)
            self.ss = self.sb(st, "ss", [128, 64], F32)
            self._vscr["gates"] = [self.sb(st, "gates%d" % v, [128, NT, 32], F32) for v in (0, 1)]
            self.lbv = self.sb(st, "lbv", [128, 4, 16], F32)
            self.ps = [st.enter_context(nc.psum_tensor("ps%d" % i, [128, 1024], F32)) for i in range(4)]
            self.consts()
            self.v = 0
            self.load_h(self.x[0])
            self.mlstm_layer("a")
            self.v = 1
            self.load_h(self.x[1])
            self.mlstm_layer("ab")
            self.ffn_layer(0, "halo")
            self.store_h()
            self.v = 0
            self.load_h(self.x[0])
            self.mlstm_layer("b")
            self.ffn_layer(0, "halo")
            self.ffn_layer(0, "main")
            self.hgrn_layer("a")
            self.store_h()
            self.v = 1
            self.load_h(self.h_d)
            self.ffn_layer(0, "main")
            self.hgrn_layer("ab")
            self.ffn_layer(1, "halo")
            self.store_h()
            self.v = 0
            self.load_h(self.h_d)
            self.hgrn_layer("b")
            self.ffn_layer(1, "halo")
            self.store_h()
            self.v = 0
            self.dyn = True
            S.want_pid = True
            self.load_h(None)
            self.ffn_layer(1, "main")
            self.final_norm()
            if self.dbg == "h":
                S.disabled = False
                S.barrier()
                for i in range(NT):
                    S.dma("sp", self.dbg_h[i * 128:(i + 1) * 128, :], self.h[:, i, :], reads=[S.B("h", i)], writes=[S.B("dbg_h")])
            S.finish()
        return nc

    _IN_SHAPES = {
        "x": [2, T, D], "norm_mix_g": [2, D], "norm_ffn_g": [2, D],
        "m_w_in": [D, 6144], "m_w_gate": [2, 128, KC * 32], "m_b_gate": [2, 32], "m_head_g": [1, D], "m_w_out": [D, D],
        "h_w_in": [D, 10240], "h_lb": [2, D], "h_head_g": [1, D], "h_w_out": [D, D],
        "f_w_up0": [D, 2 * DFF], "f_w_up1": [D, 2 * DFF], "f_conv_w0": [3, DFF], "f_conv_w1": [3, DFF],
        "f_conv_b0": [1, DFF], "f_conv_b1": [1, DFF], "f_w_down0": [DFF, D], "f_w_down1": [DFF, D],
        "final_g": [1, D],
    }

    def __getattr__(self, name):
        shapes = Prog._IN_SHAPES
        vs = self.__dict__.get("_vscr", {})
        if name in vs:
            return vs[name][self.__dict__.get("v", 0)]
        if name in ("xi1", "xi2", "xi3", "xi4"):
            return vs["xs" + name[2]][self.v ^ 1]
        if name in ("xo1", "xo2", "xo3", "xo4"):
            return vs["xs" + name[2]][self.v]
        if name in shapes:
            ap = self.nc.dram_tensor(name, list(shapes[name]), F32, kind="ExternalInput").ap()
            self.in_names.append(name)
            setattr(self, name, ap)
            return ap
        raise AttributeError(name)

    def declare_io(self):
        self.in_names = []
        self._vscr = {}
        self.y = self.dout("y", [T, D])
        self.dyn = False
        if self.dbg == "h":
            self.dbg_h = self.dout("dbg_h", [T, D])

        def vs(name, shape, dt):
            self._vscr[name] = [self.dscr("%s_v%d" % (name, v), shape, dt) for v in (0, 1)]
        vs("qT_d", [B_H, 128, T], BF16)
        vs("kT_d", [A_H, 128, T], BF16)
        vs("K_d", [T, 1024], BF16)
        vs("V_d", [T, D], BF16)
        vs("O_d", [T, D], BF16)
        vs("f1T_d", [B_H, 128, T], F32)
        vs("f2T_d", [B_H, 128, T], F32)
        self.hd_all = self.dscr("h_d_all", [2, T, D], F32)
        self._vscr["h_d"] = [self.hd_all[0], self.hd_all[1]]
        vs("yacc_d", [T, D], F32)
        vs("xs1", [128, A_H, 258], F32)
        vs("xs2", [128, KC], F32)
        vs("xs3", [128, B_H, 128], F32)
        self.xs4_all = self.dscr("xs4_all", [2, 128, KC], F32)
        self._vscr["xs4"] = [self.xs4_all[0], self.xs4_all[1]]

    def consts(self):
        S = self.S
        bc = S.B("consts")

        def tri(t, pat, op, cm, val=1.0):
            S.op("pool", lambda e: e.memset(t[:], val), writes=[bc])
            S.op("pool", lambda e: e.affine_select(t[:], t[:], pattern=pat, compare_op=op, fill=0.0, base=0, channel_multiplier=cm), reads=[bc], writes=[bc])
        tri(self.ident, [[-1, 128]], ALU.is_equal, 1)
        tri(self.identf, [[-1, 128]], ALU.is_equal, 1)
        tri(self.m_le, [[1, 128]], ALU.is_ge, -1)
        tri(self.m_ge, [[-1, 128]], ALU.is_ge, 1)
        S.op("pool", lambda e: e.memset(self.ones[:], 1.0), writes=[bc])
        self.bconst = bc

    def load_h(self, src):
        S = self.S
        if self.dyn:
            hd_all = self.hd_all
            for i in range(NT):
                S.dma("sp", None, None, reads=[S.B("h_d", 0), S.B("h_d", 1)], writes=[S.B("h", i)],
                      fn=lambda e, i=i: e.dma_start(out=self.h[:, i, :], in_=hd_all[bass.ds(S.pid % 2, 1), i * 128:(i + 1) * 128, :]))
            return
        for i in range(NT):
            S.dma("sp", self.h[:, i, :], src[i * 128:(i + 1) * 128, :], reads=[S.B("h_d", self.v)], writes=[S.B("h", i)])

    def store_h(self):
        S = self.S
        for i in range(NT):
            S.dma("sp", self.h_d[i * 128:(i + 1) * 128, :], self.h[:, i, :], reads=[S.B("h", i)], writes=[S.B("h_d", self.v)])

    def move_yacc(self, y_acc, nh, store):
        S = self.S
        for i in range(NT):
            bys = [S.B("y_acc", i, hd) for hd in range(nh)]
            if store:
                S.dma("sp", self.yacc_d[i * 128:(i + 1) * 128, :], y_acc[:, i, :], reads=bys, writes=[S.B("yacc_d", self.v)])
            else:
                S.dma("sp", y_acc[:, i, :], self.yacc_d[i * 128:(i + 1) * 128, :], reads=[S.B("yacc_d", self.v)], writes=bys)

    def load_fm_vec(self, st, name, row_ap, n, dst=None):
        S = self.S
        tmp = self.sb(st, name + "_tm", [64, 128], F32)
        if dst is None:
            dst = self.sb(st, name, [128, n], F32)
        b = S.B(name)
        bt = S.B(name + "_tm")
        S.dma("sp", tmp[0:n, :], row_ap.rearrange("o (j p) -> (o j) p", p=128), writes=[bt])
        pj = 7
        S.op("pe", lambda e: e.transpose(self.bank(pj)[:, 0:n], tmp[0:n, :], self.identf[0:n, 0:n]), reads=[bt, self.bconst], writes=[self.PB(pj)])
        S.op("dve", lambda e: e.tensor_copy(dst[:, 0:n], self.bank(pj)[:, 0:n]), reads=[self.PB(pj)], writes=[b])
        return dst, b

    def load_gbc(self, row_ap):
        S = self.S
        S.dma("sp", self.gbc[:], row_ap.to_broadcast([128, D]), writes=[S.B("gbc")])

    def norm_to_T(self, st, hnT, bT, gain_row, src=None, src_b=None, ss_off=0, tiles=None):
        S = self.S
        self.load_gbc(gain_row)
        hb = [self.sb(st, "hnb%d" % j, [128, D], BF16) for j in range(2)]
        for i in (range(NT) if tiles is None else tiles):
            j = i % 2
            bhb = S.B("hnb", j)
            bss = S.B("ss")
            src_i = self.h[:, i, :]
            bsrc = S.B("h", i)
            S.op("act", lambda e, j=j, i=i, src_i=src_i: e.activation(hb[j][:], src_i, AF.Square, accum_out=self.ss[:, ss_off + i:ss_off + i + 1]), reads=[bsrc], writes=[bhb, bss])
            S.op("act", lambda e, i=i: e.activation(self.ss[:, 32 + i:33 + i], self.ss[:, ss_off + i:ss_off + i + 1], AF.Ln, bias=EPS, scale=1.0 / D), reads=[bss], writes=[bss])
            S.op("act", lambda e, i=i: e.activation(self.ss[:, 48 + i:49 + i], self.ss[:, 32 + i:33 + i], AF.Exp, scale=-0.5), reads=[bss], writes=[bss])
            S.op("dve", lambda e, j=j, i=i, src_i=src_i: e.scalar_tensor_tensor(hb[j][:], src_i, self.ss[:, 48 + i:49 + i], self.gbc[:], ALU.mult, ALU.mult), reads=[bsrc, bss, S.B("gbc")], writes=[bhb])
            self.transpose_tile(hb[j], bhb, hnT, bT, i)

    def transpose_tile(self, src, bsrc, dstT, bT, i):
        S = self.S
        for half in range(2):
            pj = 6 + half
            pv = self.bank(pj).bitcast(BF16)
            for k8 in range(8):
                kc = half * 8 + k8
                S.op("pe", lambda e, kc=kc, k8=k8, pv=pv: e.transpose(pv[:, k8 * 128:(k8 + 1) * 128], src[:, kc * 128:(kc + 1) * 128], self.ident[:]),
                     reads=[bsrc, self.bconst], writes=[self.PB(pj)], inc=(k8 == 7))
            S.op("act" if half == 0 else "dve",
                 (lambda e, pv=pv, half=half: e.activation(dstT[:, half * 8:half * 8 + 8, i * 128:(i + 1) * 128], pv[:, :].rearrange("p (a b) -> p a b", a=8), AF.Copy)) if half == 0 else
                 (lambda e, pv=pv, half=half: e.tensor_copy(dstT[:, half * 8:half * 8 + 8, i * 128:(i + 1) * 128], pv[:, :].rearrange("p (a b) -> p a b", a=8))),
                 reads=[self.PB(pj)], writes=[bT])

    def load_w(self, slots, w_ap, c0, ncols, nk=KC):
        S = self.S
        j = self.slot_rr % len(slots)
        self.slot_rr += 1
        wv = w_ap[:, c0:c0 + ncols].rearrange("(kc p) n -> p kc n", p=128)
        b = S.B("wslot", id(slots), j)
        for kq in range(0, nk, 4):
            S.dma("pool", slots[j][:, kq:kq + 4, 0:ncols], wv[:, kq:kq + 4, :], writes=[S.B("wslot", id(slots), j, kq)], reads=[])
        return slots[j], [S.B("wslot", id(slots), j, kq) for kq in range(0, nk, 4)]

    def proj_fm(self, hnT, bT, slot, bslot, mc, banks, ncols_tok=(512, 512)):
        S = self.S
        for kc in range(KC):
            for hf in range(2):
                S.op("pe", lambda e, kc=kc, hf=hf: e.matmul(self.bank(banks[hf])[:, :], slot[:, kc, mc * 128:(mc + 1) * 128], hnT[:, kc, hf * 512:(hf + 1) * 512], start=(kc == 0), stop=(kc == KC - 1)),
                     reads=[bT, bslot[kc // 4]], writes=[self.PB(banks[hf])], inc=(kc == KC - 1 and hf == 1))

    def proj_tm(self, hnT, bT, slot, bslot, i, bank_j, ncols=512):
        S = self.S
        for kc in range(KC):
            S.op("pe", lambda e, kc=kc: e.matmul(self.bank(bank_j)[:, 0:ncols], hnT[:, kc, i * 128:(i + 1) * 128], slot[:, kc, 0:ncols], start=(kc == 0), stop=(kc == KC - 1)),
                 reads=[bT, bslot[kc // 4]], writes=[self.PB(bank_j)], inc=(kc == KC - 1))

    def mlstm_layer(self, part):
        S = self.S
        nc = self.nc
        if part in ("a", "ab"):
            self.mlstm_inproj()
        with ExitStack() as st:
            y_acc = self.sb(st, "y_acc", [128, NT, D], F32)
            if part == "b":
                self.move_yacc(y_acc, A_H, store=False)
            self.mlstm_gate_math(st)
            if part in ("a", "ab"):
                with ExitStack() as st2:
                    self.mlstm_scan(st2, y_acc, 0)
                    if part == "a":
                        self.move_yacc(y_acc, A_H, store=True)
                    S.barrier()
                if part == "a":
                    return
            with ExitStack() as st2:
                self.mlstm_scan(st2, y_acc, 1)
                S.barrier()
            yT = self.sb(st, "yT", [128, KC, T + 2], BF16)
            bT = S.B("yT")
            self.head_norm(st, y_acc, yT, bT, A_H, self.m_head_g, self.O_d, S.B("O_d"))
            S.barrier()
            self.out_proj(st, yT, bT, self.m_w_out)
            S.barrier()

    def mlstm_inproj(self):
        S = self.S
        nc = self.nc
        with ExitStack() as st:
            hnT = self.sb(st, "hnT", [128, KC, T + 2], BF16)
            bT = S.B("hnT")
            bg = S.B("gates")
            self.norm_to_T(st, hnT, bT, self.norm_mix_g[0:1, :])
            self.chk("norm0")
            slots = [self.sb(st, "wslot%d" % j, [128, KC, WB], BF16) for j in range(3)]
            self.slot_rr = 0
            stg_fm = [self.sb(st, "stgfm%d" % j, [128, T], BF16) for j in range(2)]
            stg_tm = [self.sb(st, "stgtm%d" % j, [128, NT, WB], BF16) for j in range(2)]
            wg = self.sb(st, "wgate", [128, KC, 32], BF16)
            bgb = self.sb(st, "bgate_bc", [128, 32], F32)
            S.dma("pool", wg[:], self.m_w_gate[self.v].rearrange("p (kc n) -> p kc n", kc=KC), writes=[S.B("wgate")])
            S.dma("sp", bgb[:], self.m_b_gate[self.v:self.v + 1, :].to_broadcast([128, 32]), writes=[S.B("bgate")])
            n_fm = 0
            for blk in range(4):
                slot, bslot = self.load_w(slots, self.m_w_in, blk * WB, WB)
                for mc in range(4):
                    head = (blk * 4 + mc) % 8
                    isq = blk < 2
                    banks = (0, 1) if (n_fm % 2 == 0) else (2, 3)
                    sj = n_fm % 2
                    n_fm += 1
                    self.proj_fm(hnT, bT, slot, bslot, mc, banks)
                    for hf in range(2):
                        S.op("act", lambda e, hf=hf, sj=sj, banks=banks, isq=isq: e.activation(stg_fm[sj][:, hf * 512:(hf + 1) * 512], self.bank(banks[hf])[:, :], AF.Copy, scale=(128 ** -0.5 if isq else 1.0)),
                             reads=[self.PB(banks[hf])], writes=[S.B("stgfm", sj)])
                    dst = self.qT_d if isq else self.kT_d
                    S.dma("sp", dst[head, :, :], stg_fm[sj][:], reads=[S.B("stgfm", sj)], writes=[S.B("qT_d" if isq else "kT_d", head)])
            self.chk("fm0")
            n_tm = 0
            for blk in range(2, 12):
                slot, bslot = self.load_w(slots, self.m_w_in, blk * WB, WB)
                sj = blk % 2
                kind = "k" if blk < 4 else ("v" if blk < 8 else "o")
                for i in range(NT):
                    bj = 4 + (n_tm % 2)
                    n_tm += 1
                    self.proj_tm(hnT, bT, slot, bslot, i, bj)
                    if kind == "o":
                        S.op("act", lambda e, i=i, sj=sj, bj=bj: e.activation(stg_tm[sj][:, i, :], self.bank(bj)[:, :], AF.Sigmoid),
                             reads=[self.PB(bj)], writes=[S.B("stgtm", sj)])
                    else:
                        S.op("dve", lambda e, i=i, sj=sj, bj=bj: e.tensor_copy(stg_tm[sj][:, i, :], self.bank(bj)[:, :]),
                             reads=[self.PB(bj)], writes=[S.B("stgtm", sj)])
                if kind == "k":
                    dst, c0, bd = self.K_d, (blk - 2) * WB, S.B("K_d")
                elif kind == "v":
                    dst, c0, bd = self.V_d, (blk - 4) * WB, S.B("V_d")
                else:
                    dst, c0, bd = self.O_d, (blk - 8) * WB, S.B("O_d")
                S.dma("sp", dst[:, c0:c0 + WB].rearrange("(i p) c -> p i c", p=128), stg_tm[sj][:], reads=[S.B("stgtm", sj)], writes=[bd])
                self.chk("tm0_%d" % blk)
            self.chk("tm0")
            for i in range(NT):
                bj = 4 + (i % 2)
                for kc in range(KC):
                    S.op("pe", lambda e, kc=kc, i=i, bj=bj: e.matmul(self.bank(bj)[:, 0:32], hnT[:, kc, i * 128:(i + 1) * 128], wg[:, kc, :], start=(kc == 0), stop=(kc == KC - 1)),
                         reads=[bT, S.B("wgate")], writes=[self.PB(bj)], inc=(kc == KC - 1))
                S.op("dve", lambda e, i=i, bj=bj, G=self.gates: e.tensor_tensor(G[:, i, :], self.bank(bj)[:, 0:32], bgb[:], ALU.add),
                     reads=[self.PB(bj), S.B("bgate")], writes=[bg])
            S.barrier()

    def mlstm_gate_math(self, st):
        S = self.S
        bg = S.B("gates")
        G = self.gates
        self.lf = lf = self.sb(st, "lf", [128, NT, 16], F32)
        self.gw = self.sb(st, "gw", [128, NT, 16], F32)
        self.ge = self.sb(st, "ge", [128, NT, 16], F32)
        self.gs = self.sb(st, "gs", [128, NT, 16], F32)
        self.gd = self.sb(st, "gd", [128, NT, 16], F32)
        tmp = self.sb(st, "gtmp", [128, NT, 16], F32)
        gw, ge, gs, gd = self.gw, self.ge, self.gs, self.gd
        bm = S.B("gmath")
        for d in range(2):
            S.op("act", lambda e, d=d: e.activation(lf[:, :, d * 8:(d + 1) * 8], G[:, :, 16 * d + 8:16 * d + 16], AF.Exp, scale=-1.0), reads=[bg], writes=[bm])
        S.op("act", lambda e: e.activation(lf[:], lf[:], AF.Ln, bias=1.0, scale=1.0), reads=[bm], writes=[bm])
        S.op("dve", lambda e: e.tensor_scalar(lf[:], lf[:], -1.0, None, ALU.mult), reads=[bm], writes=[bm])
        pj = 4
        pv = self.bank(pj)[:, 0:NT * 48].rearrange("p (i c) -> p i c", i=NT)
        for i in range(NT):
            S.op("pe", lambda e, i=i: e.matmul(pv[:, i, 0:16], self.m_le[:], lf[:, i, 0:16], start=True, stop=True), reads=[bm, self.bconst], writes=[self.PB(pj)], inc=False)
            S.op("pe", lambda e, i=i: e.matmul(pv[:, i, 16:32], self.m_ge[:], lf[:, i, 0:16], start=True, stop=True), reads=[bm, self.bconst], writes=[self.PB(pj)], inc=False)
            S.op("pe", lambda e, i=i: e.matmul(pv[:, i, 32:48], self.ones[:], lf[:, i, 0:16], start=True, stop=True), reads=[bm, self.bconst], writes=[self.PB(pj)], inc=True)
        bq = S.B("gq")
        bcs = self.sb(st, "gbcs", [128, NT, 16], F32)
        S.op("dve", lambda e: e.tensor_copy(bcs[:, :, 0:8], pv[:, :, 0:8]), reads=[self.PB(pj)], writes=[S.B("gbcs")])
        S.op("dve", lambda e: e.tensor_copy(bcs[:, :, 8:16], pv[:, :, 24:32]), reads=[self.PB(pj)], writes=[S.B("gbcs")])
        S.op("act", lambda e: e.activation(ge[:], bcs[:], AF.Exp), reads=[S.B("gbcs")], writes=[bq])
        S.op("act", lambda e: e.activation(gd[:], pv[:, :, 32:48], AF.Exp), reads=[self.PB(pj)], writes=[bq])
        for d in range(2):
            S.op("dve", lambda e, d=d: e.tensor_tensor(tmp[:, :, d * 8:(d + 1) * 8], G[:, :, 16 * d:16 * d + 8], bcs[:, :, d * 8:(d + 1) * 8], ALU.subtract), reads=[bg, S.B("gbcs")], writes=[S.B("gtmp")])
        S.op("act", lambda e: e.activation(gw[:], tmp[:], AF.Exp), reads=[S.B("gtmp")], writes=[bq])
        S.op("dve", lambda e: e.tensor_tensor(gs[:], gw[:], gd[:], ALU.mult), reads=[bq], writes=[bq])
        self.bgq = bq

    def mlstm_scan(self, st, y_acc, d):
        S = self.S
        bq = self.bgq
        gw, ge, gs, gd = self.gw, self.ge, self.gs, self.gd
        tiles = list(range(NT)) if d == 0 else list(range(NT - 1, -1, -1))
        mask = self.m_le if d == 0 else self.m_ge
        nm = "ms%d" % d
        bufs = []
        for j in range(2):
            qT = self.sb(st, nm + "qT%d" % j, [128, T], BF16)
            kT = self.sb(st, nm + "kT%d" % j, [128, T], BF16)
            Kt = self.sb(st, nm + "K%d" % j, [128, NT, 128], BF16)
            Vt = self.sb(st, nm + "V%d" % j, [128, NT, 258], BF16)
            C32 = [self.sb(st, nm + "C32_%d_%d" % (j, k), [128, 258], F32) for k in range(2)]
            Cb = [self.sb(st, nm + "Cb%d_%d" % (j, k), [128, 258], BF16) for k in range(2)]
            S.op("pool", lambda e, Vt=Vt: e.memset(Vt[:, :, 256:258], 1.0), writes=[S.B(nm, "V", j)])
            for k in range(2):
                S.op("pool", lambda e, t=C32[k]: e.memset(t[:], 0.0), writes=[S.B(nm, "C", j, k)])
                S.op("pool", lambda e, t=Cb[k]: e.memset(t[:], 0.0), writes=[S.B(nm, "C", j, k)])
            bufs.append((qT, kT, Kt, Vt, C32, Cb))
        PT = [self.sb(st, nm + "PT%d" % j, [128, 128], BF16) for j in range(2)]
        Kw = [self.sb(st, nm + "Kw%d" % j, [128, 128], BF16) for j in range(2)]
        sc2 = self.sb(st, nm + "sc", [128, 2, 8], F32)

        def rr(gens):
            gens = list(gens)
            while gens:
                for g in list(gens):
                    try:
                        next(g)
                    except StopIteration:
                        gens.remove(g)

        def head_gen(hd):
            j = hd % 2
            sc = sc2[:, j, :]
            qT, kT, Kt, Vt, C32s, Cbs = bufs[j]
            bq_, bk_, bK_, bV_ = (S.B(nm, "q", j), S.B(nm, "k", j), S.B(nm, "K", j), S.B(nm, "V", j))
            bCs = [S.B(nm, "C", j, 0), S.B(nm, "C", j, 1)]
            cur = 0
            C32, Cb, bC = C32s[0], Cbs[0], bCs[0]
            S.dma("sp", qT[:], self.qT_d[hd, :, :], reads=[S.B("qT_d", hd)], writes=[bq_])
            S.dma("sp", kT[:], self.kT_d[hd, :, :], reads=[S.B("kT_d", hd)], writes=[bk_])
            S.dma("sp", Kt[:], self.K_d[:, hd * 128:(hd + 1) * 128].rearrange("(i p) c -> p i c", p=128), reads=[S.B("K_d")], writes=[bK_])
            S.dma("sp", Vt[:, :, 0:256], self.V_d[:, hd * 256:(hd + 1) * 256].rearrange("(i p) c -> p i c", p=128), reads=[S.B("V_d")], writes=[bV_])
            if d == 1:
                S.dma("sp", C32[:], self.xi1[:, hd, :], reads=[S.B("xs1", self.v ^ 1)], writes=[bC])
                S.op("act", lambda e, Cb=Cb, C32=C32: e.activation(Cb[:, 0:258], C32[:], AF.Copy), reads=[bC], writes=[bC])
            first = True
            col = d * 8 + hd
            yield
            for i in tiles:
                p = j
                pS, pO, pU = (0, 2, 4) if p == 0 else (1, 3, 5)
                tsl = slice(i * 128, (i + 1) * 128)
                S.op("pe", lambda e, pS=pS, kT=kT, qT=qT, tsl=tsl: e.matmul(self.bank(pS)[:, 0:128], kT[:, tsl], qT[:, tsl], start=True, stop=True),
                     reads=[bk_, bq_], writes=[self.PB(pS)])
                S.op("dve", lambda e, pS=pS, p=p, i=i, col=col: e.scalar_tensor_tensor(PT[p][:], self.bank(pS)[:, 0:128], gw[:, i, col:col + 1], mask[:], ALU.mult, ALU.mult),
                     reads=[self.PB(pS), bq, self.bconst], writes=[S.B(nm, "PT", p)])
                S.op("pool", lambda e, p=p, i=i, col=col, Kt=Kt: e.tensor_scalar(Kw[p][:], Kt[:, i, :], gs[:, i, col:col + 1], None, ALU.mult),
                     reads=[bK_, bq], writes=[S.B(nm, "Kw", p)])
                nostate = first and d == 0
                S.op("pe", lambda e, pO=pO, p=p, i=i, Vt=Vt, nostate=nostate: e.matmul(self.bank(pO)[:, 0:258], PT[p][:], Vt[:, i, 0:258], start=True, stop=nostate),
                     reads=[S.B(nm, "PT", p), bV_], writes=[self.PB(pO)], inc=nostate)
                if not nostate:
                    S.op("pe", lambda e, pO=pO, qT=qT, tsl=tsl, Cb=Cb: e.matmul(self.bank(pO)[:, 0:258], qT[:, tsl], Cb[:, 0:258], start=False, stop=True),
                         reads=[bq_, bC], writes=[self.PB(pO)])
                S.op("pe", lambda e, pU=pU, p=p, i=i, Vt=Vt: e.matmul(self.bank(pU)[:, 0:258], Kw[p][:], Vt[:, i, 0:258], start=True, stop=True),
                     reads=[S.B(nm, "Kw", p), bV_], writes=[self.PB(pU)])
                bsc = S.B(nm, "sc", j)
                S.op("dve", lambda e, pO=pO, i=i, col=col: e.tensor_tensor(sc[:, 0:1], self.bank(pO)[:, 256:257], ge[:, i, col:col + 1], ALU.mult),
                     reads=[self.PB(pO), bq], writes=[bsc])
                S.op("dve", lambda e: e.scalar_tensor_tensor(sc[:, 3:4], sc[:, 0:1], -1.0, sc[:, 0:1], ALU.mult, ALU.max), reads=[bsc], writes=[bsc])
                S.op("dve", lambda e: e.tensor_scalar(sc[:, 1:2], sc[:, 3:4], 1.0, None, ALU.max), reads=[bsc], writes=[bsc])
                S.op("dve", lambda e: e.reciprocal(sc[:, 4:5], sc[:, 1:2]), reads=[bsc], writes=[bsc])
                S.op("dve", lambda e, i=i, col=col: e.tensor_tensor(sc[:, 2:3], ge[:, i, col:col + 1], sc[:, 4:5], ALU.mult), reads=[bsc, bq], writes=[bsc])
                ysl = y_acc[:, i, hd * 256:(hd + 1) * 256]
                by = S.B("y_acc", i, hd)
                if d == 0:
                    S.op("act", lambda e, pO=pO, ysl=ysl: e.activation(ysl, self.bank(pO)[:, 0:256], AF.Copy, scale=sc[:, 2:3]),
                         reads=[self.PB(pO), bsc], writes=[by])
                else:
                    S.op("dve", lambda e, pO=pO, ysl=ysl: e.scalar_tensor_tensor(ysl, self.bank(pO)[:, 0:256], sc[:, 2:3], ysl, ALU.mult, ALU.add),
                         reads=[self.PB(pO), bsc, by], writes=[by])
                nxt = 1 - cur
                C32n, Cbn, bCn = C32s[nxt], Cbs[nxt], bCs[nxt]
                if nostate:
                    S.op("dve", lambda e, pU=pU, C32n=C32n: e.tensor_copy(C32n[:], self.bank(pU)[:, 0:258]), reads=[self.PB(pU)], writes=[bCn])
                else:
                    S.op("dve", lambda e, pU=pU, C32=C32, C32n=C32n, i=i, col=col: e.scalar_tensor_tensor(C32n[:], C32[:], gd[:, i, col:col + 1], self.bank(pU)[:, 0:258], ALU.mult, ALU.add),
                         reads=[self.PB(pU), bq, bC], writes=[bCn])
                S.op("act", lambda e, Cbn=Cbn, C32n=C32n: e.activation(Cbn[:, 0:258], C32n[:], AF.Copy), reads=[bCn], writes=[bCn])
                cur = nxt
                C32, Cb, bC = C32n, Cbn, bCn
                first = False
                yield
            if d == 0:
                S.dma("sp", self.xo1[:, hd, :], C32[:], reads=[bC], writes=[S.B("xs1", self.v)])

        for hd in range(0, A_H, 2):
            rr([head_gen(hd), head_gen(hd + 1)])

    def head_norm(self, st, y_acc, yT, bT, nh, head_g_row, gate_d, bgate_d):
        S = self.S
        dh = D // nh
        self.load_gbc(head_g_row)
        with ExitStack() as s2:
            sq = self.sb(s2, "hn_sq", [128, D], F32)
            gt = [self.sb(s2, "hn_gt%d" % j, [128, D], BF16) for j in range(2)]
            gso = self.sb(s2, "hn_gso", [128, D], BF16)
            yb = [self.sb(s2, "hn_yb%d" % j, [128, D], BF16) for j in range(2)]
            ssh = self.sb(s2, "hn_ss", [128, 3, 16], F32)
            for i in range(NT):
                j = i % 2
                bys = [S.B("y_acc", i, hd) for hd in range(nh)]
                S.dma("sp", gt[j][:], gate_d[i * 128:(i + 1) * 128, :], reads=[bgate_d], writes=[S.B("hn_gt", j)])
                S.op("pool", lambda e, i=i: e.tensor_tensor(sq[:], y_acc[:, i, :], y_acc[:, i, :], ALU.mult), reads=bys, writes=[S.B("hn_sq")])
                S.op("dve", lambda e: e.tensor_reduce(ssh[:, 0, 0:nh], sq[:].rearrange("p (h d) -> p h d", h=nh), AX.X, ALU.add), reads=[S.B("hn_sq")], writes=[S.B("hn_ss")])
                S.op("act", lambda e: e.activation(ssh[:, 1, 0:nh], ssh[:, 0, 0:nh], AF.Ln, bias=EPS, scale=1.0 / dh), reads=[S.B("hn_ss")], writes=[S.B("hn_ss")])
                S.op("act", lambda e: e.activation(ssh[:, 2, 0:nh], ssh[:, 1, 0:nh], AF.Exp, scale=-0.5), reads=[S.B("hn_ss")], writes=[S.B("hn_ss")])
                S.op("pool", lambda e, j=j: e.tensor_tensor(gso[:], gt[j][:], self.gbc[:], ALU.mult), reads=[S.B("hn_gt", j), S.B("gbc")], writes=[S.B("hn_gso")])
                S.op("dve", lambda e, i=i: e.tensor_tensor(sq[:].rearrange("p (h d) -> p h d", h=nh), y_acc[:, i, :].rearrange("p (h d) -> p h d", h=nh),
                                                          ssh[:, 2, 0:nh].unsqueeze(2).to_broadcast([128, nh, dh]), ALU.mult),
                     reads=bys + [S.B("hn_ss")], writes=[S.B("hn_sq")])
                S.op("dve", lambda e, j=j: e.tensor_tensor(yb[j][:], sq[:], gso[:], ALU.mult), reads=[S.B("hn_sq"), S.B("hn_gso")], writes=[S.B("hn_yb", j)])
                self.transpose_tile(yb[j], S.B("hn_yb", j), yT, bT, i)

    def out_proj(self, st, yT, bT, w_ap):
        S = self.S
        OB = 256
        with ExitStack() as s2:
            slots = [self.sb(s2, "owslot%d" % j, [128, KC, OB], BF16) for j in range(3)]
            self.slot_rr = 0
            n = 0
            for blk in range(D // OB):
                slot, bslot = self.load_w(slots, w_ap, blk * OB, OB)
                for i in range(NT):
                    bj = n % 4
                    n += 1
                    self.proj_tm(yT, bT, slot, bslot, i, bj, ncols=OB)
                    hs = self.h[:, i, blk * OB:(blk + 1) * OB]
                    S.op("dve", lambda e, hs=hs, bj=bj: e.tensor_tensor(hs, hs, self.bank(bj)[:, 0:OB], ALU.add),
                         reads=[self.PB(bj), S.B("h", i)], writes=[S.B("h", i)])

    def ffn_layer(self, l, part):
        S = self.S
        xo = self.xo2 if l == 0 else self.xo4
        xi = self.xi2 if l == 0 else self.xi4
        xkey = "xs2" if l == 0 else "xs4"
        with ExitStack() as st:
            hnT = self.sb(st, "fhnT%d" % l, [128, KC, T + 2], BF16)
            bT = S.B("fhnT")
            self.norm_to_T(st, hnT, bT, self.norm_ffn_g[l:l + 1, :], tiles=([NT - 1] if part == "halo" else None))
            hal = self.sb(st, "halo%d" % l, [128, KC], F32)
            if part == "halo":
                S.op("dve", lambda e: e.tensor_copy(hal[:], hnT[:, :, T - 1]), reads=[bT], writes=[S.B("halo")])
                S.dma("sp", xo[:, :], hal[:], reads=[S.B("halo")], writes=[S.B(xkey, self.v)])
                S.barrier()
                return
            hal2 = self.sb(st, "halo2_%d" % l, [128, KC], F32)
            if self.dyn:
                xs4_all = self.xs4_all
                S.dma("sp", None, None, reads=[S.B(xkey, 0), S.B(xkey, 1)], writes=[S.B("halo2")],
                      fn=lambda e: e.dma_start(out=hal2[:], in_=xs4_all[bass.ds((S.pid + 1) % 2, 1), :, :]))
            else:
                S.dma("sp", hal2[:], xi[:, :], reads=[S.B(xkey, self.v ^ 1)], writes=[S.B("halo2")])
            S.op("dve", lambda e: e.tensor_copy(hnT[:, :, T], hal2[:]), reads=[S.B("halo2"), bT], writes=[bT])
            cw = self.sb(st, "cw%d" % l, [128, 4, 64], F32)
            bcw = []
            for j in range(3):
                jj = j if (self.v == 0 or self.dyn) else 2 - j
                _, b = self.load_fm_vec(st, "cw%d_%d" % (l, j), getattr(self, "f_conv_w%d" % l)[jj:jj + 1, :], NFF, dst=cw[:, j, :])
                bcw.append(b)
            _, b = self.load_fm_vec(st, "cb%d" % l, getattr(self, "f_conv_b%d" % l)[0:1, :], NFF, dst=cw[:, 3, :])
            bcw.append(b)
            GR = 3
            slots = [self.sb(st, "fwslot%d_%d" % (l, j), [128, KC, GR * 128], BF16) for j in range(3)]
            self.slot_rr = 0
            wd = [self.sb(st, "wd%d_%d" % (l, j), [128, D], BF16) for j in range(2 * GR)]
            gT = [self.sb(st, "gT%d_%d" % (l, j), [128, T], BF16) for j in range(2 * GR)]
            csb = self.sb(st, "csb%d" % l, [128, T], F32)
            gel = self.sb(st, "gel%d" % l, [128, T], BF16)
            w_up = getattr(self, "f_w_up%d" % l)
            w_down = getattr(self, "f_w_down%d" % l)
            ngroups = (NFF + GR - 1) // GR
            pa = self.ps[0]
            pv = self.ps[1]
            ndp = 0
            for g in range(ngroups):
                c_lo = g * GR
                nch = min(GR, NFF - c_lo)
                ncol = nch * 128
                sa, bsa = self.load_w(slots, w_up, c_lo * 128, ncol)
                sv, bsv = self.load_w(slots, w_up, DFF + c_lo * 128, ncol)
                ring = (g % 2) * GR
                for mc in range(nch):
                    c = c_lo + mc
                    S.dma("pool", wd[ring + mc][:], w_down[c * 128:(c + 1) * 128, :], writes=[S.B("wd", l, ring + mc)])
                for mc in range(nch):
                    c = c_lo + mc
                    for kc in range(KC):
                        lhs = sa[:, kc, mc * 128:(mc + 1) * 128]
                        S.op("pe", lambda e, lhs=lhs, kc=kc: e.matmul(pa[:, 0:512], lhs, hnT[:, kc, 0:512], start=(kc == 0), stop=(kc == KC - 1)),
                             reads=[bT, bsa[kc // 4]], writes=[self.PB(0)], inc=False)
                        S.op("pe", lambda e, lhs=lhs, kc=kc: e.matmul(pa[:, 512:1024], lhs, hnT[:, kc, 512:1024], start=(kc == 0), stop=(kc == KC - 1)),
                             reads=[bT, bsa[kc // 4]], writes=[self.PB(1)], inc=False)
                        S.op("pe", lambda e, lhs=lhs, kc=kc: e.matmul(self.bank(4)[:, 0:1], lhs, hnT[:, kc, T:T + 1], start=(kc == 0), stop=(kc == KC - 1)),
                             reads=[bT, bsa[kc // 4]], writes=[self.PB(4)], inc=(kc == KC - 1))
                    for kc in range(KC):
                        lhs = sv[:, kc, mc * 128:(mc + 1) * 128]
                        S.op("pe", lambda e, lhs=lhs, kc=kc: e.matmul(pv[:, 0:512], lhs, hnT[:, kc, 0:512], start=(kc == 0), stop=(kc == KC - 1)),
                             reads=[bT, bsv[kc // 4]], writes=[self.PB(2)], inc=False)
                        S.op("pe", lambda e, lhs=lhs, kc=kc: e.matmul(pv[:, 512:1024], lhs, hnT[:, kc, 512:1024], start=(kc == 0), stop=(kc == KC - 1)),
                             reads=[bT, bsv[kc // 4]], writes=[self.PB(3)], inc=(kc == KC - 1))
                    bcs = S.B("csb")
                    pab = [self.PB(0), self.PB(1)]
                    S.op("dve", lambda e, c=c: e.tensor_scalar(csb[:], pa[:, 0:T], cw[:, 1, c:c + 1], cw[:, 3, c:c + 1], ALU.mult, ALU.add),
                         reads=pab + bcw, writes=[bcs])
                    S.op("dve", lambda e, c=c: e.scalar_tensor_tensor(csb[:, 1:T], pa[:, 0:T - 1], cw[:, 0, c:c + 1], csb[:, 1:T], ALU.mult, ALU.add),
                         reads=pab + bcw + [bcs], writes=[bcs])
                    S.op("dve", lambda e, c=c: e.scalar_tensor_tensor(csb[:, 0:T - 1], pa[:, 1:T], cw[:, 2, c:c + 1], csb[:, 0:T - 1], ALU.mult, ALU.add),
                         reads=pab + bcw + [bcs], writes=[bcs])
                    S.op("dve", lambda e, c=c: e.scalar_tensor_tensor(csb[:, T - 1:T], self.bank(4)[:, 0:1], cw[:, 2, c:c + 1], csb[:, T - 1:T], ALU.mult, ALU.add),
                         reads=[self.PB(4)] + bcw + [bcs], writes=[bcs])
                    S.op("act", lambda e: e.activation(gel[:], csb[:], AF.Gelu), reads=[bcs], writes=[S.B("gel")])
                    S.op("dve", lambda e, mc=mc, ring=ring: e.tensor_tensor(gT[ring + mc][:], gel[:], pv[:, 0:T], ALU.mult),
                         reads=[S.B("gel"), self.PB(2), self.PB(3)], writes=[S.B("gT", l, ring + mc)])
                for i in range(NT):
                    for nb in range(4):
                        bj = 5 + (ndp % 3)
                        ndp += 1
                        for mc in range(nch):
                            S.op("pe", lambda e, mc=mc, ring=ring, i=i, nb=nb, bj=bj, nch=nch: e.matmul(self.bank(bj)[:, :], gT[ring + mc][:, i * 128:(i + 1) * 128], wd[ring + mc][:, nb * 512:(nb + 1) * 512], start=(mc == 0), stop=(mc == nch - 1)),
                                 reads=[S.B("gT", l, ring + mc), S.B("wd", l, ring + mc)], writes=[self.PB(bj)], inc=(mc == nch - 1))
                        hs = self.h[:, i, nb * 512:(nb + 1) * 512]
                        S.op("dve", lambda e, hs=hs, bj=bj: e.tensor_tensor(hs, hs, self.bank(bj)[:, :], ALU.add),
                             reads=[self.PB(bj), S.B("h", i)], writes=[S.B("h", i)])
            S.barrier()

    def hgrn_layer(self, part):
        S = self.S
        if part in ("a", "ab"):
            self.hgrn_inproj()
        with ExitStack() as st:
            y_acc = self.sb(st, "y2_acc", [128, NT, D], F32)
            if part in ("a", "ab"):
                with ExitStack() as st2:
                    self.hgrn_scan(st2, y_acc, 0)
                    if part == "a":
                        self.move_yacc(y_acc, B_H, store=True)
                    S.barrier()
                if part == "a":
                    return
            if part == "b":
                self.move_yacc(y_acc, B_H, store=False)
            with ExitStack() as st2:
                self.hgrn_scan(st2, y_acc, 1)
                S.barrier()
            yT = self.sb(st, "hyT", [128, KC, T + 2], BF16)
            bT = S.B("hyT")
            self.head_norm(st, y_acc, yT, bT, B_H, self.h_head_g, self.O_d, S.B("O_d"))
            S.barrier()
            self.out_proj(st, yT, bT, self.h_w_out)
            S.barrier()

    def hgrn_inproj(self):
        S = self.S
        lbv = self.lbv
        blb = S.B("lbv")
        with ExitStack() as st:
            hnT = self.sb(st, "hhnT", [128, KC, T + 2], BF16)
            bT = S.B("hhnT")
            self.norm_to_T(st, hnT, bT, self.norm_mix_g[1:2, :])
            l0, b0 = self.load_fm_vec(st, "hlb0", self.h_lb[0:1, :], 16)
            l1, b1 = self.load_fm_vec(st, "hlb1", self.h_lb[1:2, :], 16)
            S.op("dve", lambda e: e.tensor_tensor(lbv[:, 3, :], l1[:, 0:16], l0[:, 0:16], ALU.subtract), reads=[b0, b1], writes=[blb])
            S.op("act", lambda e: e.activation(lbv[:, 0, :], lbv[:, 3, :], AF.Sigmoid), reads=[blb], writes=[blb])
            S.op("dve", lambda e: e.tensor_scalar(lbv[:, 2, :], lbv[:, 0, :], -1.0, None, ALU.add), reads=[blb], writes=[blb])
            S.op("dve", lambda e: e.tensor_scalar(lbv[:, 1, :], lbv[:, 2, :], -1.0, None, ALU.mult), reads=[blb], writes=[blb])
            slots = [self.sb(st, "hwslot%d" % j, [128, KC, WB], BF16) for j in range(3)]
            self.slot_rr = 0
            stg_q = [self.sb(st, "hstgq%d" % j, [128, T], BF16) for j in range(2)]
            stg_f = [self.sb(st, "hstgf%d" % j, [128, T], F32) for j in range(2)]
            stg_tm = [self.sb(st, "hstgtm%d" % j, [128, NT, WB], BF16) for j in range(2)]
            n_fm = 0
            fb1, fb2 = (12, 16) if self.v == 0 else (16, 12)
            for kind, blk0, dst in (("q", 0, self.qT_d), ("f1", fb1, self.f1T_d), ("f2", fb2, self.f2T_d)):
                for b4 in range(4):
                    slot, bslot = self.load_w(slots, self.h_w_in, (blk0 + b4) * WB, WB)
                    for mc in range(4):
                        head = b4 * 4 + mc
                        banks = (0, 1) if (n_fm % 2 == 0) else (2, 3)
                        sj = n_fm % 2
                        n_fm += 1
                        self.proj_fm(hnT, bT, slot, bslot, mc, banks)
                        stg = stg_q[sj] if kind == "q" else stg_f[sj]
                        bst = S.B("hstg", kind == "q", sj)
                        for hf in range(2):
                            if hf == 0:
                                S.op("act", lambda e, hf=hf, stg=stg, banks=banks: e.activation(stg[:, hf * 512:(hf + 1) * 512], self.bank(banks[hf])[:, :], AF.Copy),
                                     reads=[self.PB(banks[hf])], writes=[bst])
                            else:
                                S.op("dve", lambda e, hf=hf, stg=stg, banks=banks: e.tensor_copy(stg[:, hf * 512:(hf + 1) * 512], self.bank(banks[hf])[:, :]),
                                     reads=[self.PB(banks[hf])], writes=[bst])
                        S.dma("sp", dst[head, :, :], stg[:], reads=[bst], writes=[S.B("h_" + kind + "_d", head)])
            n_tm = 0
            for blk in range(4, 12):
                slot, bslot = self.load_w(slots, self.h_w_in, blk * WB, WB)
                sj = blk % 2
                kind = "v" if blk < 8 else "g"
                for i in range(NT):
                    bj = 4 + (n_tm % 2)
                    n_tm += 1
                    self.proj_tm(hnT, bT, slot, bslot, i, bj)
                    if kind == "g":
                        S.op("act", lambda e, i=i, sj=sj, bj=bj: e.activation(stg_tm[sj][:, i, :], self.bank(bj)[:, :], AF.Silu),
                             reads=[self.PB(bj)], writes=[S.B("hstgtm", sj)])
                    else:
                        S.op("dve", lambda e, i=i, sj=sj, bj=bj: e.tensor_copy(stg_tm[sj][:, i, :], self.bank(bj)[:, :]),
                             reads=[self.PB(bj)], writes=[S.B("hstgtm", sj)])
                if kind == "v":
                    dst, c0, bd = self.V_d, (blk - 4) * WB, S.B("V_d")
                else:
                    dst, c0, bd = self.O_d, (blk - 8) * WB, S.B("O_d")
                S.dma("sp", dst[:, c0:c0 + WB].rearrange("(i p) c -> p i c", p=128), stg_tm[sj][:], reads=[S.B("hstgtm", sj)], writes=[bd])
            S.barrier()

    def hgrn_scan(self, st, y_acc, d):
        S = self.S
        lbv = self.lbv
        blb = S.B("lbv")
        nm = "hs%d" % d
        tiles = list(range(NT)) if d == 0 else list(range(NT - 1, -1, -1))
        mask = self.m_le if d == 0 else self.m_ge
        fT_d = self.f1T_d if d == 0 else self.f2T_d
        qT_d, V_d, xi3, xo3 = self.qT_d, self.V_d, self.xi3, self.xo3
        vv = self.v
        fkind = "f1" if d == 0 else "f2"
        ins = []
        qh1 = self.sb(st, nm + "qh", [128, T], BF16)
        fh1 = self.sb(st, nm + "fh", [128, T], F32)
        for j in range(2):
            Vh = self.sb(st, nm + "Vh%d" % j, [128, NT, 128], BF16)
            ins.append((qh1, fh1, Vh))
        t_sig, t_f, t_P, t_k = [self.sb(st, nm + "t%d" % k, [128, T], F32) for k in range(4)]
        b_sig, b_f, b_P, b_k = [S.B(nm, "t", k) for k in range(4)]
        rst = self.sb(st, nm + "rst", [128, T], BF16)
        prods2, Kt2, dec2 = [], [], []
        for jj in range(2):
            prods = [self.sb(st, nm + "pr%d_%d" % (k, jj), [128, T], BF16) for k in range(6)]
            for pr in prods:
                S.op("pool", lambda e, pr=pr: e.memset(pr[:], 0.0), writes=[S.B(nm, "prods", jj)])
            prods2.append(prods)
            Kt2.append((self.sb(st, nm + "KAt%d" % jj, [128, NT, 128], BF16), self.sb(st, nm + "KBt%d" % jj, [128, NT, 128], BF16)))
            dec2.append(self.sb(st, nm + "dec%d" % jj, [128, 16], F32))
        PT = [self.sb(st, nm + "PT%d" % j, [128, 128], BF16) for j in range(2)]
        S32 = self.sb(st, nm + "S32", [128, 128], F32)
        S32b = self.sb(st, nm + "S32b", [128, 128], F32)
        Sb = [self.sb(st, nm + "Sb%d" % j, [128, 128], BF16) for j in range(2)]
        S.op("pool", lambda e: e.memset(rst[:], 1.0), writes=[S.B(nm, "rst")])
        S.op("pool", lambda e: e.memset(rst[:].rearrange("p (c l) -> p c l", l=64)[:, :, 0:1], 0.0), writes=[S.B(nm, "rst")])

        def ev(x):
            return x[:].rearrange("p (c two l) -> p c two l", two=2, l=64)[:, :, 0, :]

        def od(x):
            return x[:].rearrange("p (c two l) -> p c two l", two=2, l=64)[:, :, 1, :]

        def c3(x):
            return x[:].rearrange("p (c l) -> p c l", l=64)

        def gm(m):
            j = m % 2
            qh, fh, Vh = ins[j]
            qA, qB, kA, kB, xA, xB = prods2[j]
            KAt, KBt = Kt2[j]
            dec = dec2[j]
            bpr, bdec, bKt = S.B(nm, "prods", j), S.B(nm, "dec", j), S.B(nm, "Kt", j)
            bq_, bf_, bV_ = S.B(nm, "qh"), S.B(nm, "fh"), S.B(nm, "Vh", j)
            S.dma("sp", qh[:], qT_d[m, :, :], reads=[S.B("h_q_d", m)], writes=[bq_])
            S.dma("sp", fh[:], fT_d[m, :, :], reads=[S.B("h_" + fkind + "_d", m)], writes=[bf_])
            S.dma("sp", Vh[:], V_d[:, m * 128:(m + 1) * 128].rearrange("(i p) c -> p i c", p=128), reads=[S.B("V_d")], writes=[bV_])
            yield
            S.op("act", lambda e: e.activation(t_sig[:], fh[:], AF.Sigmoid), reads=[bf_], writes=[b_sig])
            yield
            S.op("dve", lambda e: e.tensor_scalar(t_f[:], t_sig[:], lbv[:, 1, m:m + 1], lbv[:, 0, m:m + 1], ALU.mult, ALU.add), reads=[b_sig, blb], writes=[b_f])
            yield
            S.op("pool", lambda e: e.tensor_scalar(t_k[:], t_sig[:], lbv[:, 2, m:m + 1], lbv[:, 1, m:m + 1], ALU.mult, ALU.add), reads=[b_sig, blb], writes=[b_k])
            yield
            S.op("act", lambda e: e.activation(t_f[:], t_f[:], AF.Ln), reads=[b_f], writes=[b_f])
            yield
            S.op("dve", lambda e: e.tensor_tensor_scan(t_P[:], rst[:], t_f[:], 0.0, ALU.mult, ALU.add), reads=[b_f, S.B(nm, "rst")], writes=[b_P])
            yield
            S.op("act", lambda e: e.activation(dec[:], c3(t_P)[:, :, 63], AF.Exp), reads=[b_P], writes=[bdec])
            yield
            if d == 0:
                E, bE = t_P, b_P
                sq, sk = 1.0, -1.0
            else:
                S.op("pool", lambda e: e.tensor_tensor(t_sig[:], t_P[:], t_f[:], ALU.subtract), reads=[b_P, b_f], writes=[b_sig])
                yield
                E, bE = t_sig, b_sig
                sq, sk = -1.0, 1.0
            t_e1, b_e1 = t_f, b_f
            t_e2, b_e2 = E, bE
            S.op("act", lambda e: e.activation(t_e1[:], E[:], AF.Exp, scale=sq), reads=[bE], writes=[b_e1])
            yield
            S.op("act", lambda e: e.activation(t_e2[:], E[:], AF.Exp, scale=sk), reads=[bE], writes=[b_e2])
            yield
            S.op("dve", lambda e: e.tensor_tensor(ev(qA), ev(qh), ev(t_e1), ALU.mult), reads=[bq_, b_e1], writes=[bpr])
            yield
            S.op("dve", lambda e: e.tensor_tensor(od(qB), od(qh), od(t_e1), ALU.mult), reads=[bq_, b_e1], writes=[bpr])
            yield
            S.op("pool", lambda e: e.tensor_tensor(ev(kA), ev(t_k), ev(t_e2), ALU.mult), reads=[b_k, b_e2], writes=[bpr])
            yield
            S.op("pool", lambda e: e.tensor_tensor(od(kB), od(t_k), od(t_e2), ALU.mult), reads=[b_k, b_e2], writes=[bpr])
            yield
            srcA, srcB = (kA, kB) if d == 0 else (qA, qB)
            decA = dec[:, :].rearrange("p (c two) -> p c two", two=2)[:, :, 0:1].to_broadcast([128, 8, 64])
            decB = dec[:, :].rearrange("p (c two) -> p c two", two=2)[:, :, 1:2].to_broadcast([128, 8, 64])
            S.op("pool", lambda e: e.tensor_tensor(ev(xA), ev(srcA), decA, ALU.mult), reads=[bpr, bdec], writes=[bpr])
            yield
            S.op("pool", lambda e: e.tensor_tensor(od(xB), od(srcB), decB, ALU.mult), reads=[bpr, bdec], writes=[bpr])
            yield
            tA, tB = (xA, xB) if d == 0 else (kA, kB)
            for (src, dstT, pj) in ((tA, KAt, 6), (tB, KBt, 7)):
                pvb = self.bank(pj).bitcast(BF16)
                for i in range(NT):
                    S.op("pe", lambda e, src=src, i=i, pvb=pvb: e.transpose(pvb[:, i * 128:(i + 1) * 128], src[:, i * 128:(i + 1) * 128], self.ident[:]),
                         reads=[bpr, self.bconst], writes=[self.PB(pj)], inc=(i == NT - 1))
                S.op("act", lambda e, dstT=dstT, pvb=pvb: e.activation(dstT[:], pvb[:, :].rearrange("p (a b) -> p a b", a=NT), AF.Copy),
                     reads=[self.PB(pj)], writes=[bKt])
                yield

        def tl(m):
            j = m % 2
            qh, fh, Vh = ins[j]
            qA, qB, kA, kB, xA, xB = prods2[j]
            KAt, KBt = Kt2[j]
            dec = dec2[j]
            bpr, bdec, bKt = S.B(nm, "prods", j), S.B(nm, "dec", j), S.B(nm, "Kt", j)
            bV_ = S.B(nm, "Vh", j)
            bS = S.B(nm, "S")
            QA, QB = (qA, qB) if d == 0 else (xA, xB)
            if d == 0:
                S.op("pool", lambda e: e.memset(S32[:], 0.0), writes=[bS])
            else:
                S.dma("sp", S32[:], xi3[:, m, :], reads=[S.B("xs3", vv ^ 1)], writes=[bS])
            S.op("pool", lambda e: e.tensor_copy(Sb[0][:], S32[:]), reads=[bS], writes=[S.B(nm, "Sb", 0)])
            yield
            for n_it, i in enumerate(tiles):
                p = n_it % 2
                pA, pO = (0, 2) if p == 0 else (1, 3)
                pUx, pUy = 4, 5
                tsl = slice(i * 128, (i + 1) * 128)
                cX, cY = (2 * i, 2 * i + 1) if d == 0 else (2 * i + 1, 2 * i)
                KX, KY = (KAt, KBt) if d == 0 else (KBt, KAt)
                QX, QY = (QA, QB) if d == 0 else (QB, QA)
                S.op("pe", lambda e, pA=pA, tsl=tsl: e.matmul(self.bank(pA)[:, 0:128], kA[:, tsl], qA[:, tsl], start=True, stop=False),
                     reads=[bpr], writes=[self.PB(pA)], inc=False)
                S.op("pe", lambda e, pA=pA, tsl=tsl: e.matmul(self.bank(pA)[:, 0:128], kB[:, tsl], qB[:, tsl], start=False, stop=True),
                     reads=[bpr], writes=[self.PB(pA)])
                S.op("dve", lambda e, pA=pA, p=p: e.tensor_tensor(PT[p][:], self.bank(pA)[:, 0:128], mask[:], ALU.mult),
                     reads=[self.PB(pA), self.bconst], writes=[S.B(nm, "PT", p)])
                S.op("pe", lambda e, i=i, KX=KX: e.matmul(self.bank(pUx)[:, 0:128], KX[:, i, :], Vh[:, i, :], start=True, stop=True),
                     reads=[bKt, bV_], writes=[self.PB(pUx)])
                S.op("pe", lambda e, i=i, KY=KY: e.matmul(self.bank(pUy)[:, 0:128], KY[:, i, :], Vh[:, i, :], start=True, stop=True),
                     reads=[bKt, bV_], writes=[self.PB(pUy)])
                S.op("pe", lambda e, pO=pO, p=p, i=i: e.matmul(self.bank(pO)[:, 0:128], PT[p][:], Vh[:, i, :], start=True, stop=False),
                     reads=[S.B(nm, "PT", p), bV_], writes=[self.PB(pO)], inc=False)
                S.op("pe", lambda e, pO=pO, QX=QX, tsl=tsl: e.matmul(self.bank(pO)[:, 0:128], QX[:, tsl], Sb[0][:], start=False, stop=False),
                     reads=[bpr, S.B(nm, "Sb", 0)], writes=[self.PB(pO)], inc=False)
                bS2 = S.B(nm, "S2")
                S.op("dve", lambda e, cX=cX: e.scalar_tensor_tensor(S32b[:], S32[:], dec[:, cX:cX + 1], self.bank(pUx)[:, 0:128], ALU.mult, ALU.add),
                     reads=[self.PB(pUx), bdec, bS], writes=[bS2])
                S.op("pool", lambda e: e.tensor_copy(Sb[1][:], S32b[:]), reads=[bS2], writes=[S.B(nm, "Sb", 1)])
                S.op("pe", lambda e, pO=pO, QY=QY, tsl=tsl: e.matmul(self.bank(pO)[:, 0:128], QY[:, tsl], Sb[1][:], start=False, stop=True),
                     reads=[bpr, S.B(nm, "Sb", 1)], writes=[self.PB(pO)])
                S.op("dve", lambda e, cY=cY: e.scalar_tensor_tensor(S32[:], S32b[:], dec[:, cY:cY + 1], self.bank(pUy)[:, 0:128], ALU.mult, ALU.add),
                     reads=[self.PB(pUy), bdec, bS2], writes=[bS])
                S.op("pool", lambda e: e.tensor_copy(Sb[0][:], S32[:]), reads=[bS], writes=[S.B(nm, "Sb", 0)])
                ysl = y_acc[:, i, m * 128:(m + 1) * 128]
                by = S.B("y_acc", i, m)
                if d == 0:
                    S.op("dve", lambda e, pO=pO, ysl=ysl: e.tensor_copy(ysl, self.bank(pO)[:, 0:128]), reads=[self.PB(pO)], writes=[by])
                else:
                    S.op("dve", lambda e, pO=pO, ysl=ysl: e.tensor_tensor(ysl, ysl, self.bank(pO)[:, 0:128], ALU.add), reads=[self.PB(pO), by], writes=[by])
                yield
            if d == 0:
                S.dma("sp", xo3[:, m, :], S32[:], reads=[bS], writes=[S.B("xs3", vv)])

        def drain(g):
            for _ in g:
                pass

        drain(gm(0))
        for m in range(B_H):
            a = tl(m)
            b = gm(m + 1) if m + 1 < B_H else iter(())
            done_a = done_b = False
            while not (done_a and done_b):
                if not done_a:
                    try:
                        next(a)
                    except StopIteration:
                        done_a = True
                for _ in range(2):
                    if not done_b:
                        try:
                            next(b)
                        except StopIteration:
                            done_b = True

    def final_norm(self):
        S = self.S
        self.load_gbc(self.final_g[0:1, :])
        with ExitStack() as st:
            ob = [self.sb(st, "fo%d" % j, [128, D], F32) for j in range(2)]
            for i in range(NT):
                j = i % 2
                bss = S.B("ss")
                S.op("act", lambda e, j=j, i=i: e.activation(ob[j][:], self.h[:, i, :], AF.Square, accum_out=self.ss[:, i:i + 1]), reads=[S.B("h", i)], writes=[S.B("fo", j), bss])
                S.op("act", lambda e, i=i: e.activation(self.ss[:, 32 + i:33 + i], self.ss[:, i:i + 1], AF.Ln, bias=EPS, scale=1.0 / D), reads=[bss], writes=[bss])
                S.op("act", lambda e, i=i: e.activation(self.ss[:, 48 + i:49 + i], self.ss[:, 32 + i:33 + i], AF.Exp, scale=-0.5), reads=[bss], writes=[bss])
                S.op("dve", lambda e, j=j, i=i: e.scalar_tensor_tensor(ob[j][:], self.h[:, i, :], self.ss[:, 48 + i:49 + i], self.gbc[:], ALU.mult, ALU.mult), reads=[S.B("h", i), bss, S.B("gbc")], writes=[S.B("fo", j)])
                S.dma("sp", self.y[i * 128:(i + 1) * 128, :], ob[j][:], reads=[S.B("fo", j)], writes=[S.B("y")])


def _fm(w):
    n = w.shape[1]
    return np.ascontiguousarray(w.reshape(KC, 128, n).transpose(1, 0, 2).reshape(128, KC * n))


def _prep(inputs, n_cores=8):
    f = lambda k: np.ascontiguousarray(np.asarray(inputs[k], dtype=np.float32))
    x = f("x")
    w_in = f("mlstm_w_in")[0]
    bg = f("mlstm_b_gate")[0]
    perm_g = np.concatenate([np.arange(16, 32), np.arange(0, 16)])
    wg = w_in[:, 6144:6176]
    common = {
        "norm_mix_g": f("norm_mix_g"), "norm_ffn_g": f("norm_ffn_g"),
        "m_w_in": np.ascontiguousarray(w_in[:, :6144]),
        "m_w_gate": np.stack([_fm(wg), _fm(wg[:, perm_g])]),
        "m_b_gate": np.stack([bg, bg[perm_g]]),
        "m_head_g": f("mlstm_head_g").reshape(1, D), "m_w_out": f("mlstm_w_out")[0],
        "h_w_in": f("hgrn_w_in")[0],
        "h_lb": f("hgrn_lb"), "h_head_g": f("hgrn_head_g").reshape(1, D), "h_w_out": f("hgrn_w_out")[0],
        "f_w_up0": f("ffn_w_up")[0], "f_w_up1": f("ffn_w_up")[1],
        "f_conv_w0": f("ffn_conv_w")[0], "f_conv_w1": f("ffn_conv_w")[1],
        "f_conv_b0": f("ffn_conv_b")[0:1], "f_conv_b1": f("ffn_conv_b")[1:2],
        "f_w_down0": f("ffn_w_down")[0], "f_w_down1": f("ffn_w_down")[1],
        "final_g": f("final_g").reshape(1, D),
    }
    nb = x.shape[0]
    maps = []
    cw1 = common["f_conv_w1"]
    for c in range(n_cores):
        b = (c // 2) % nb
        m = dict(common)
        if c % 2 == 1:
            m["f_conv_w1"] = np.ascontiguousarray(cw1[::-1])
        m["x"] = np.ascontiguousarray(np.stack([x[b, :T], x[b, T:][::-1]]))
        maps.append(m)
    return maps


def _run(maps, dbg=None, core_ids=None, stop=None):
    if core_ids is None:
        core_ids = list(range(len(maps)))
    prog = Prog(dbg=dbg, stop=stop, n_cores=len(core_ids))
    nc = prog.build()
    in_maps = [{name: maps[c][name] for name in prog.in_names} for c in core_ids]
    res = run_bass_kernel_spmd(nc, in_maps, core_ids=list(range(len(core_ids))))
    return {c: res.results[j] for j, c in enumerate(core_ids)}


def kernel(**inputs):
    maps = _prep(inputs)
    nb = np.asarray(inputs["x"]).shape[0]
    out = _run(maps)
    y = np.empty((nb, 2 * T, D), np.float32)
    for b in range(nb):
        y[b, :T] = out[2 * b]["y"]
        y[b, T:] = out[2 * b + 1]["y"][::-1]
    return y
```

```python
import numpy as np
from contextlib import ExitStack
import concourse.bass as bass
import concourse.mybir as mybir
from concourse.bass_utils import run_bass_kernel_spmd

F32 = mybir.dt.float32
BF16 = mybir.dt.bfloat16
AF = mybir.ActivationFunctionType
ALU = mybir.AluOpType
AX = mybir.AxisListType

T = 1024
NT = 8
D = 2048
KC = 16
DFF = 5504
NFF = 43
EPS = 1e-6
A_H = 8
B_H = 16
WB = 512


class Buf:
    __slots__ = ("name", "w", "r")

    def __init__(self, name=""):
        self.name = name
        self.w = None
        self.r = {}


class Sched:
    ENGS = ("pe", "act", "dve", "pool", "sp")

    def __init__(self, nc, stack, n_dma_sems=12, same_engine_sync=True):
        self.nc = nc
        self.same_engine_sync = same_engine_sync
        self.thunks = {e: [] for e in self.ENGS}
        self.sems = {}
        self.count = {}
        self.waited = {e: {} for e in self.ENGS}
        for e in self.ENGS:
            self.sems[e] = stack.enter_context(nc.semaphore("s_" + e))
            self.count[e] = 0
        self.dma_sems = {}
        self.dma_rr = {}
        for q in ("sp", "pool", "act"):
            lst = []
            for i in range(n_dma_sems):
                key = "d_%s_%d" % (q, i)
                self.sems[key] = stack.enter_context(nc.semaphore(key))
                self.count[key] = 0
                lst.append(key)
            self.dma_sems[q] = lst
            self.dma_rr[q] = 0
        self.n_inst = 0
        self.bufs = {}
        self.disabled = False
        self.want_pid = False
        self.pid = None

    def B(self, *key):
        b = self.bufs.get(key)
        if b is None:
            b = Buf(str(key))
            self.bufs[key] = b
        return b

    def _deps(self, eng, reads, writes):
        deps = {}

        def add(k, v):
            if v > deps.get(k, 0):
                deps[k] = v
        for b in reads:
            if b.w is not None:
                add(*b.w)
        for b in writes:
            if b.w is not None:
                add(*b.w)
            for k, v in b.r.items():
                add(k, v)
        out = []
        for k, v in deps.items():
            if k == eng and (eng == "pe" or not self.same_engine_sync):
                continue
            if self.waited[eng].get(k, 0) >= v:
                continue
            self.waited[eng][k] = v
            out.append((k, v))
        return out

    def _emit_waits(self, eng, waits):
        sems = self.sems
        for k, v in waits:
            self.thunks[eng].append(lambda e, k=k, v=v: e.wait_ge(sems[k], v))

    def op(self, eng, fn, reads=(), writes=(), inc=True):
        if self.disabled:
            return
        waits = self._deps(eng, reads, writes)
        self._emit_waits(eng, waits)
        val = self.count[eng] + 1
        if inc:
            self.count[eng] = val
            sem = self.sems[eng]
            self.thunks[eng].append(lambda e: fn(e).then_inc(sem, 1))
        else:
            self.thunks[eng].append(lambda e: fn(e))
        for b in reads:
            if b.r.get(eng, 0) < val:
                b.r[eng] = val
        for b in writes:
            b.w = (eng, val)
            b.r = {}
        self.n_inst += 1

    def dma(self, q, out, in_, reads=(), writes=(), fn=None, **kw):
        if self.disabled:
            return
        i = self.dma_rr[q]
        self.dma_rr[q] = (i + 1) % len(self.dma_sems[q])
        key = self.dma_sems[q][i]
        waits = self._deps(q, reads, writes)
        prev = self.count[key]
        if prev > 0 and self.waited[q].get(key, 0) < prev:
            self.waited[q][key] = prev
            waits.append((key, prev))
        self._emit_waits(q, waits)
        val = prev + 16
        self.count[key] = val
        sem = self.sems[key]
        if fn is None:
            self.thunks[q].append(
                lambda e: e.dma_start(out=out, in_=in_, **kw).then_inc(sem, 16))
        else:
            self.thunks[q].append(lambda e: fn(e).then_inc(sem, 16))
        for b in reads:
            if b.r.get(key, 0) < val:
                b.r[key] = val
        for b in writes:
            b.w = (key, val)
            b.r = {}
        self.n_inst += 1

    def barrier(self):
        if self.disabled:
            return
        keys = list(self.count.keys())
        for e in self.ENGS:
            waits = []
            for k in keys:
                v = self.count[k]
                if k == e or v == 0:
                    continue
                if self.waited[e].get(k, 0) >= v:
                    continue
                self.waited[e][k] = v
                waits.append((k, v))
            self._emit_waits(e, waits)

    def finish(self):
        self.barrier()
        nc = self.nc
        th = self.thunks
        with nc.Block() as block:
            @block.tensor
            def _(e):
                for t in th["pe"]:
                    t(e)

            @block.scalar
            def _(e):
                for t in th["act"]:
                    t(e)

            @block.vector
            def _(e):
                for t in th["dve"]:
                    t(e)

            @block.gpsimd
            def _(e):
                for t in th["pool"]:
                    t(e)

            @block.sync
            def _(e):
                if self.want_pid:
                    self.pid = e.partition_id()
                for t in th["sp"]:
                    t(e)


class StopBuild(Exception):
    pass


class Prog:
    def __init__(self, n_stages=5, dbg=None, stop=None, n_cores=8):
        self.n_cores = n_cores
        self.stop = stop
        self.v = 0
        self.n_stages = n_stages
        self.dbg = dbg
        self.nc = bass.Bass("TRN2", target_bir_lowering=False, num_devices=self.n_cores)
        self.out_names = []

    def din(self, name, shape, dt=F32):
        return self.nc.dram_tensor(name, list(shape), dt, kind="ExternalInput").ap()

    def dout(self, name, shape, dt=F32):
        self.out_names.append(name)
        return self.nc.dram_tensor(name, list(shape), dt, kind="ExternalOutput").ap()

    def dscr(self, name, shape, dt):
        return self.nc.dram_tensor(name, list(shape), dt, kind="Internal").ap()

    def sb(self, st, name, shape, dt):
        self.uid = getattr(self, "uid", 0) + 1
        return st.enter_context(self.nc.sbuf_tensor("%s_u%d" % (name, self.uid), list(shape), dt))

    def bank(self, j):
        return self.ps[j // 2][:, (j % 2) * 512:(j % 2 + 1) * 512]

    def PB(self, j):
        return self.S.B("psum", j)

    def chk(self, tag):
        if self.stop == tag:
            self.S.barrier()
            self.S.disabled = True

    def build(self):
        nc = self.nc
        with ExitStack() as st:
            self.st = st
            S = self.S = Sched(nc, st)
            self.declare_io()
            self.h = self.sb(st, "h", [128, NT, D], F32)
            self.gbc = self.sb(st, "gbc", [128, D], F32)
            self.ident = self.sb(st, "ident", [128, 128], BF16)
            self.identf = self.sb(st, "identf", [128, 128], F32)
            self.m_le = self.sb(st, "m_le", [128, 128], F32)
            self.m_ge = self.sb(st, "m_ge", [128, 128], F32)
            self.ones = self.sb(st, "ones", [128, 128], F32)
            self.ss = self.sb(st, "ss", [128, 64], F32)
            self._vscr["gates"] = [self.sb(st, "gates%d" % v, [128, NT, 32], F32) for v in (0, 1)]
            self.lbv = self.sb(st, "lbv", [128, 4, 16], F32)
            self.ps = [st.enter_context(nc.psum_tensor("ps%d" % i, [128, 1024], F32)) for i in range(4)]
            self.consts()
            self.v = 0
            self.load_h(self.x[0])
            self.mlstm_layer("a")
            self.v = 1
            self.load_h(self.x[1])
            self.mlstm_layer("ab")
            self.ffn_layer(0, "halo")
            self.store_h()
            self.v = 0
            self.load_h(self.x[0])
            self.mlstm_layer("b")
            self.ffn_layer(0, "halo")
            self.ffn_layer(0, "main")
            self.hgrn_layer("a")
            self.store_h()
            self.v = 1
            self.load_h(self.h_d)
            self.ffn_layer(0, "main")
            self.hgrn_layer("ab")
            self.ffn_layer(1, "halo")
            self.store_h()
            self.v = 0
            self.load_h(self.h_d)
            self.hgrn_layer("b")
            self.ffn_layer(1, "halo")
            self.store_h()
            self.v = 0
            self.dyn = True
            S.want_pid = True
            self.load_h(None)
            self.ffn_layer(1, "main")
            self.final_norm()
            if self.dbg == "h":
                S.disabled = False
                S.barrier()
                for i in range(NT):
                    S.dma("sp", self.dbg_h[i * 128:(i + 1) * 128, :], self.h[:, i, :], reads=[S.B("h", i)], writes=[S.B("dbg_h")])
            S.finish()
        return nc

    _IN_SHAPES = {
        "x": [2, T, D], "norm_mix_g": [2, D], "norm_ffn_g": [2, D],
        "m_w_in": [D, 6144], "m_w_gate": [2, 128, KC * 32], "m_b_gate": [2, 32], "m_head_g": [1, D], "m_w_out": [D, D],
        "h_w_in": [D, 10240], "h_lb": [2, D], "h_head_g": [1, D], "h_w_out": [D, D],
        "f_w_up0": [D, 2 * DFF], "f_w_up1": [D, 2 * DFF], "f_conv_w0": [3, DFF], "f_conv_w1": [3, DFF],
        "f_conv_b0": [1, DFF], "f_conv_b1": [1, DFF], "f_w_down0": [DFF, D], "f_w_down1": [DFF, D],
        "final_g": [1, D],
    }

    def __getattr__(self, name):
        shapes = Prog._IN_SHAPES
        vs = self.__dict__.get("_vscr", {})
        if name in vs:
            return vs[name][self.__dict__.get("v", 0)]
        if name in ("xi1", "xi2", "xi3", "xi4"):
            return vs["xs" + name[2]][self.v ^ 1]
        if name in ("xo1", "xo2", "xo3", "xo4"):
            return vs["xs" + name[2]][self.v]
        if name in shapes:
            ap = self.nc.dram_tensor(name, list(shapes[name]), F32, kind="ExternalInput").ap()
            self.in_names.append(name)
            setattr(self, name, ap)
            return ap
        raise AttributeError(name)

    def declare_io(self):
        self.in_names = []
        self._vscr = {}
        self.y = self.dout("y", [T, D])
        self.dyn = False
        if self.dbg == "h":
            self.dbg_h = self.dout("dbg_h", [T, D])

        def vs(name, shape, dt):
            self._vscr[name] = [self.dscr("%s_v%d" % (name, v), shape, dt) for v in (0, 1)]
        vs("qT_d", [B_H, 128, T], BF16)
        vs("kT_d", [A_H, 128, T], BF16)
        vs("K_d", [T, 1024], BF16)
        vs("V_d", [T, D], BF16)
        vs("O_d", [T, D], BF16)
        vs("f1T_d", [B_H, 128, T], F32)
        vs("f2T_d", [B_H, 128, T], F32)
        self.hd_all = self.dscr("h_d_all", [2, T, D], F32)
        self._vscr["h_d"] = [self.hd_all[0], self.hd_all[1]]
        vs("yacc_d", [T, D], F32)
        vs("xs1", [128, A_H, 258], F32)
        vs("xs2", [128, KC], F32)
        vs("xs3", [128, B_H, 128], F32)
        self.xs4_all = self.dscr("xs4_all", [2, 128, KC], F32)
        self._vscr["xs4"] = [self.xs4_all[0], self.xs4_all[1]]

    def consts(self):
        S = self.S
        bc = S.B("consts")

        def tri(t, pat, op, cm, val=1.0):
            S.op("pool", lambda e: e.memset(t[:], val), writes=[bc])
            S.op("pool", lambda e: e.affine_select(t[:], t[:], pattern=pat, compare_op=op, fill=0.0, base=0, channel_multiplier=cm), reads=[bc], writes=[bc])
        tri(self.ident, [[-1, 128]], ALU.is_equal, 1)
        tri(self.identf, [[-1, 128]], ALU.is_equal, 1)
        tri(self.m_le, [[1, 128]], ALU.is_ge, -1)
        tri(self.m_ge, [[-1, 128]], ALU.is_ge, 1)
        S.op("pool", lambda e: e.memset(self.ones[:], 1.0), writes=[bc])
        self.bconst = bc

    def load_h(self, src):
        S = self.S
        if self.dyn:
            hd_all = self.hd_all
            for i in range(NT):
                S.dma("sp", None, None, reads=[S.B("h_d", 0), S.B("h_d", 1)], writes=[S.B("h", i)],
                      fn=lambda e, i=i: e.dma_start(out=self.h[:, i, :], in_=hd_all[bass.ds(S.pid % 2, 1), i * 128:(i + 1) * 128, :]))
            return
        for i in range(NT):
            S.dma("sp", self.h[:, i, :], src[i * 128:(i + 1) * 128, :], reads=[S.B("h_d", self.v)], writes=[S.B("h", i)])

    def store_h(self):
        S = self.S
        for i in range(NT):
            S.dma("sp", self.h_d[i * 128:(i + 1) * 128, :], self.h[:, i, :], reads=[S.B("h", i)], writes=[S.B("h_d", self.v)])

    def move_yacc(self, y_acc, nh, store):
        S = self.S
        for i in range(NT):
            bys = [S.B("y_acc", i, hd) for hd in range(nh)]
            if store:
                S.dma("sp", self.yacc_d[i * 128:(i + 1) * 128, :], y_acc[:, i, :], reads=bys, writes=[S.B("yacc_d", self.v)])
            else:
                S.dma("sp", y_acc[:, i, :], self.yacc_d[i * 128:(i + 1) * 128, :], reads=[S.B("yacc_d", self.v)], writes=bys)

    def load_fm_vec(self, st, name, row_ap, n, dst=None):
        S = self.S
        tmp = self.sb(st, name + "_tm", [64, 128], F32)
        if dst is None:
            dst = self.sb(st, name, [128, n], F32)
        b = S.B(name)
        bt = S.B(name + "_tm")
        S.dma("sp", tmp[0:n, :], row_ap.rearrange("o (j p) -> (o j) p", p=128), writes=[bt])
        pj = 7
        S.op("pe", lambda e: e.transpose(self.bank(pj)[:, 0:n], tmp[0:n, :], self.identf[0:n, 0:n]), reads=[bt, self.bconst], writes=[self.PB(pj)])
        S.op("dve", lambda e: e.tensor_copy(dst[:, 0:n], self.bank(pj)[:, 0:n]), reads=[self.PB(pj)], writes=[b])
        return dst, b

    def load_gbc(self, row_ap):
        S = self.S
        S.dma("sp", self.gbc[:], row_ap.to_broadcast([128, D]), writes=[S.B("gbc")])

    def norm_to_T(self, st, hnT, bT, gain_row, src=None, src_b=None, ss_off=0, tiles=None):
        S = self.S
        self.load_gbc(gain_row)
        hb = [self.sb(st, "hnb%d" % j, [128, D], BF16) for j in range(2)]
        for i in (range(NT) if tiles is None else tiles):
            j = i % 2
            bhb = S.B("hnb", j)
            bss = S.B("ss")
            src_i = self.h[:, i, :]
            bsrc = S.B("h", i)
            S.op("act", lambda e, j=j, i=i, src_i=src_i: e.activation(hb[j][:], src_i, AF.Square, accum_out=self.ss[:, ss_off + i:ss_off + i + 1]), reads=[bsrc], writes=[bhb, bss])
            S.op("act", lambda e, i=i: e.activation(self.ss[:, 32 + i:33 + i], self.ss[:, ss_off + i:ss_off + i + 1], AF.Ln, bias=EPS, scale=1.0 / D), reads=[bss], writes=[bss])
            S.op("act", lambda e, i=i: e.activation(self.ss[:, 48 + i:49 + i], self.ss[:, 32 + i:33 + i], AF.Exp, scale=-0.5), reads=[bss], writes=[bss])
            S.op("dve", lambda e, j=j, i=i, src_i=src_i: e.scalar_tensor_tensor(hb[j][:], src_i, self.ss[:, 48 + i:49 + i], self.gbc[:], ALU.mult, ALU.mult), reads=[bsrc, bss, S.B("gbc")], writes=[bhb])
            self.transpose_tile(hb[j], bhb, hnT, bT, i)

    def transpose_tile(self, src, bsrc, dstT, bT, i):
        S = self.S
        for half in range(2):
            pj = 6 + half
            pv = self.bank(pj).bitcast(BF16)
            for k8 in range(8):
                kc = half * 8 + k8
                S.op("pe", lambda e, kc=kc, k8=k8, pv=pv: e.transpose(pv[:, k8 * 128:(k8 + 1) * 128], src[:, kc * 128:(kc + 1) * 128], self.ident[:]),
                     reads=[bsrc, self.bconst], writes=[self.PB(pj)], inc=(k8 == 7))
            S.op("act" if half == 0 else "dve",
                 (lambda e, pv=pv, half=half: e.activation(dstT[:, half * 8:half * 8 + 8, i * 128:(i + 1) * 128], pv[:, :].rearrange("p (a b) -> p a b", a=8), AF.Copy)) if half == 0 else
                 (lambda e, pv=pv, half=half: e.tensor_copy(dstT[:, half * 8:half * 8 + 8, i * 128:(i + 1) * 128], pv[:, :].rearrange("p (a b) -> p a b", a=8))),
                 reads=[self.PB(pj)], writes=[bT])

    def load_w(self, slots, w_ap, c0, ncols, nk=KC):
        S = self.S
        j = self.slot_rr % len(slots)
        self.slot_rr += 1
        wv = w_ap[:, c0:c0 + ncols].rearrange("(kc p) n -> p kc n", p=128)
        b = S.B("wslot", id(slots), j)
        for kq in range(0, nk, 4):
            S.dma("pool", slots[j][:, kq:kq + 4, 0:ncols], wv[:, kq:kq + 4, :], writes=[S.B("wslot", id(slots), j, kq)], reads=[])
        return slots[j], [S.B("wslot", id(slots), j, kq) for kq in range(0, nk, 4)]

    def proj_fm(self, hnT, bT, slot, bslot, mc, banks, ncols_tok=(512, 512)):
        S = self.S
        for kc in range(KC):
            for hf in range(2):
                S.op("pe", lambda e, kc=kc, hf=hf: e.matmul(self.bank(banks[hf])[:, :], slot[:, kc, mc * 128:(mc + 1) * 128], hnT[:, kc, hf * 512:(hf + 1) * 512], start=(kc == 0), stop=(kc == KC - 1)),
                     reads=[bT, bslot[kc // 4]], writes=[self.PB(banks[hf])], inc=(kc == KC - 1 and hf == 1))

    def proj_tm(self, hnT, bT, slot, bslot, i, bank_j, ncols=512):
        S = self.S
        for kc in range(KC):
            S.op("pe", lambda e, kc=kc: e.matmul(self.bank(bank_j)[:, 0:ncols], hnT[:, kc, i * 128:(i + 1) * 128], slot[:, kc, 0:ncols], start=(kc == 0), stop=(kc == KC - 1)),
                 reads=[bT, bslot[kc // 4]], writes=[self.PB(bank_j)], inc=(kc == KC - 1))

    def mlstm_layer(self, part):
        S = self.S
        nc = self.nc
        if part in ("a", "ab"):
            self.mlstm_inproj()
        with ExitStack() as st:
            y_acc = self.sb(st, "y_acc", [128, NT, D], F32)
            if part == "b":
                self.move_yacc(y_acc, A_H, store=False)
            self.mlstm_gate_math(st)
            if part in ("a", "ab"):
                with ExitStack() as st2:
                    self.mlstm_scan(st2, y_acc, 0)
                    if part == "a":
                        self.move_yacc(y_acc, A_H, store=True)
                    S.barrier()
                if part == "a":
                    return
            with ExitStack() as st2:
                self.mlstm_scan(st2, y_acc, 1)
                S.barrier()
            yT = self.sb(st, "yT", [128, KC, T + 2], BF16)
            bT = S.B("yT")
            self.head_norm(st, y_acc, yT, bT, A_H, self.m_head_g, self.O_d, S.B("O_d"))
            S.barrier()
            self.out_proj(st, yT, bT, self.m_w_out)
            S.barrier()

    def mlstm_inproj(self):
        S = self.S
        nc = self.nc
        with ExitStack() as st:
            hnT = self.sb(st, "hnT", [128, KC, T + 2], BF16)
            bT = S.B("hnT")
            bg = S.B("gates")
            self.norm_to_T(st, hnT, bT, self.norm_mix_g[0:1, :])
            self.chk("norm0")
            slots = [self.sb(st, "wslot%d" % j, [128, KC, WB], BF16) for j in range(3)]
            self.slot_rr = 0
            stg_fm = [self.sb(st, "stgfm%d" % j, [128, T], BF16) for j in range(2)]
            stg_tm = [self.sb(st, "stgtm%d" % j, [128, NT, WB], BF16) for j in range(2)]
            wg = self.sb(st, "wgate", [128, KC, 32], BF16)
            bgb = self.sb(st, "bgate_bc", [128, 32], F32)
            S.dma("pool", wg[:], self.m_w_gate[self.v].rearrange("p (kc n) -> p kc n", kc=KC), writes=[S.B("wgate")])
            S.dma("sp", bgb[:], self.m_b_gate[self.v:self.v + 1, :].to_broadcast([128, 32]), writes=[S.B("bgate")])
            n_fm = 0
            for blk in range(4):
                slot, bslot = self.load_w(slots, self.m_w_in, blk * WB, WB)
                for mc in range(4):
                    head = (blk * 4 + mc) % 8
                    isq = blk < 2
                    banks = (0, 1) if (n_fm % 2 == 0) else (2, 3)
                    sj = n_fm % 2
                    n_fm += 1
                    self.proj_fm(hnT, bT, slot, bslot, mc, banks)
                    for hf in range(2):
                        S.op("act", lambda e, hf=hf, sj=sj, banks=banks, isq=isq: e.activation(stg_fm[sj][:, hf * 512:(hf + 1) * 512], self.bank(banks[hf])[:, :], AF.Copy, scale=(128 ** -0.5 if isq else 1.0)),
                             reads=[self.PB(banks[hf])], writes=[S.B("stgfm", sj)])
                    dst = self.qT_d if isq else self.kT_d
                    S.dma("sp", dst[head, :, :], stg_fm[sj][:], reads=[S.B("stgfm", sj)], writes=[S.B("qT_d" if isq else "kT_d", head)])
            self.chk("fm0")
            n_tm = 0
            for blk in range(2, 12):
                slot, bslot = self.load_w(slots, self.m_w_in, blk * WB, WB)
                sj = blk % 2
                kind = "k" if blk < 4 else ("v" if blk < 8 else "o")
                for i in range(NT):
                    bj = 4 + (n_tm % 2)
                    n_tm += 1
                    self.proj_tm(hnT, bT, slot, bslot, i, bj)
                    if kind == "o":
                        S.op("act", lambda e, i=i, sj=sj, bj=bj: e.activation(stg_tm[sj][:, i, :], self.bank(bj)[:, :], AF.Sigmoid),
                             reads=[self.PB(bj)], writes=[S.B("stgtm", sj)])
                    else:
                        S.op("dve", lambda e, i=i, sj=sj, bj=bj: e.tensor_copy(stg_tm[sj][:, i, :], self.bank(bj)[:, :]),
                             reads=[self.PB(bj)], writes=[S.B("stgtm", sj)])
                if kind == "k":
                    dst, c0, bd = self.K_d, (blk - 2) * WB, S.B("K_d")
                elif kind == "v":
                    dst, c0, bd = self.V_d, (blk - 4) * WB, S.B("V_d")
                else:
                    dst, c0, bd = self.O_d, (blk - 8) * WB, S.B("O_d")
                S.dma("sp", dst[:, c0:c0 + WB].rearrange("(i p) c -> p i c", p=128), stg_tm[sj][:], reads=[S.B("stgtm", sj)], writes=[bd])
                self.chk("tm0_%d" % blk)
            self.chk("tm0")
            for i in range(NT):
                bj = 4 + (i % 2)
                for kc in range(KC):
                    S.op("pe", lambda e, kc=kc, i=i, bj=bj: e.matmul(self.bank(bj)[:, 0:32], hnT[:, kc, i * 128:(i + 1) * 128], wg[:, kc, :], start=(kc == 0), stop=(kc == KC - 1)),
                         reads=[bT, S.B("wgate")], writes=[self.PB(bj)], inc=(kc == KC - 1))
                S.op("dve", lambda e, i=i, bj=bj, G=self.gates: e.tensor_tensor(G[:, i, :], self.bank(bj)[:, 0:32], bgb[:], ALU.add),
                     reads=[self.PB(bj), S.B("bgate")], writes=[bg])
            S.barrier()

    def mlstm_gate_math(self, st):
        S = self.S
        bg = S.B("gates")
        G = self.gates
        self.lf = lf = self.sb(st, "lf", [128, NT, 16], F32)
        self.gw = self.sb(st, "gw", [128, NT, 16], F32)
        self.ge = self.sb(st, "ge", [128, NT, 16], F32)
        self.gs = self.sb(st, "gs", [128, NT, 16], F32)
        self.gd = self.sb(st, "gd", [128, NT, 16], F32)
        self.gei = self.sb(st, "gei", [128, NT, 16], F32)
        tmp = self.sb(st, "gtmp", [128, NT, 16], F32)
        gw, ge, gs, gd, gei = self.gw, self.ge, self.gs, self.gd, self.gei
        bm = S.B("gmath")
        for d in range(2):
            S.op("act", lambda e, d=d: e.activation(lf[:, :, d * 8:(d + 1) * 8], G[:, :, 16 * d + 8:16 * d + 16], AF.Exp, scale=-1.0), reads=[bg], writes=[bm])
        S.op("act", lambda e: e.activation(lf[:], lf[:], AF.Ln, bias=1.0, scale=1.0), reads=[bm], writes=[bm])
        S.op("dve", lambda e: e.tensor_scalar(lf[:], lf[:], -1.0, None, ALU.mult), reads=[bm], writes=[bm])
        pj = 4
        pv = self.bank(pj)[:, 0:NT * 48].rearrange("p (i c) -> p i c", i=NT)
        for i in range(NT):
            S.op("pe", lambda e, i=i: e.matmul(pv[:, i, 0:16], self.m_le[:], lf[:, i, 0:16], start=True, stop=True), reads=[bm, self.bconst], writes=[self.PB(pj)], inc=False)
            S.op("pe", lambda e, i=i: e.matmul(pv[:, i, 16:32], self.m_ge[:], lf[:, i, 0:16], start=True, stop=True), reads=[bm, self.bconst], writes=[self.PB(pj)], inc=False)
            S.op("pe", lambda e, i=i: e.matmul(pv[:, i, 32:48], self.ones[:], lf[:, i, 0:16], start=True, stop=True), reads=[bm, self.bconst], writes=[self.PB(pj)], inc=True)
        bq = S.B("gq")
        bcs = self.sb(st, "gbcs", [128, NT, 16], F32)
        S.op("dve", lambda e: e.tensor_copy(bcs[:, :, 0:8], pv[:, :, 0:8]), reads=[self.PB(pj)], writes=[S.B("gbcs")])
        S.op("dve", lambda e: e.tensor_copy(bcs[:, :, 8:16], pv[:, :, 24:32]), reads=[self.PB(pj)], writes=[S.B("gbcs")])
        S.op("act", lambda e: e.activation(ge[:], bcs[:], AF.Exp), reads=[S.B("gbcs")], writes=[bq])
        S.op("act", lambda e: e.activation(gei[:], bcs[:], AF.Exp, scale=-1.0), reads=[S.B("gbcs")], writes=[bq])
        S.op("act", lambda e: e.activation(gd[:], pv[:, :, 32:48], AF.Exp), reads=[self.PB(pj)], writes=[bq])
        for d in range(2):
            S.op("dve", lambda e, d=d: e.tensor_tensor(tmp[:, :, d * 8:(d + 1) * 8], G[:, :, 16 * d:16 * d + 8], bcs[:, :, d * 8:(d + 1) * 8], ALU.subtract), reads=[bg, S.B("gbcs")], writes=[S.B("gtmp")])
        S.op("act", lambda e: e.activation(gw[:], tmp[:], AF.Exp), reads=[S.B("gtmp")], writes=[bq])
        S.op("dve", lambda e: e.tensor_tensor(gs[:], gw[:], gd[:], ALU.mult), reads=[bq], writes=[bq])
        self.bgq = bq

    def mlstm_scan(self, st, y_acc, d):
        S = self.S
        bq = self.bgq
        gw, ge, gs, gd, gei = self.gw, self.ge, self.gs, self.gd, self.gei
        tiles = list(range(NT)) if d == 0 else list(range(NT - 1, -1, -1))
        mask = self.m_le if d == 0 else self.m_ge
        nm = "ms%d" % d
        bufs = []
        for j in range(2):
            qT = self.sb(st, nm + "qT%d" % j, [128, T], BF16)
            kT = self.sb(st, nm + "kT%d" % j, [128, T], BF16)
            Kt = self.sb(st, nm + "K%d" % j, [128, NT, 128], BF16)
            Vt = self.sb(st, nm + "V%d" % j, [128, NT, 258], BF16)
            C32 = [self.sb(st, nm + "C32_%d_%d" % (j, k), [128, 258], F32) for k in range(2)]
            Cb = [self.sb(st, nm + "Cb%d_%d" % (j, k), [128, 258], BF16) for k in range(2)]
            S.op("pool", lambda e, Vt=Vt: e.memset(Vt[:, :, 256:258], 1.0), writes=[S.B(nm, "V", j)])
            for k in range(2):
                S.op("pool", lambda e, t=C32[k]: e.memset(t[:], 0.0), writes=[S.B(nm, "C", j, k)])
                S.op("pool", lambda e, t=Cb[k]: e.memset(t[:], 0.0), writes=[S.B(nm, "C", j, k)])
            bufs.append((qT, kT, Kt, Vt, C32, Cb))
        PT = [self.sb(st, nm + "PT%d" % j, [128, 128], BF16) for j in range(2)]
        Kw = [self.sb(st, nm + "Kw%d" % j, [128, 128], BF16) for j in range(2)]
        sc2 = self.sb(st, nm + "sc", [128, 2, 8], F32)
        tmpy = [self.sb(st, nm + "tmpy%d" % j, [128, 256], F32) for j in range(2)]

        def rr(gens):
            gens = list(gens)
            while gens:
                for g in list(gens):
                    try:
                        next(g)
                    except StopIteration:
                        gens.remove(g)

        def head_gen(hd):
            j = hd % 2
            sc = sc2[:, j, :]
            qT, kT, Kt, Vt, C32s, Cbs = bufs[j]
            bq_, bk_, bK_, bV_ = (S.B(nm, "q", j), S.B(nm, "k", j), S.B(nm, "K", j), S.B(nm, "V", j))
            bCs = [S.B(nm, "C", j, 0), S.B(nm, "C", j, 1)]
            cur = 0
            C32, Cb, bC = C32s[0], Cbs[0], bCs[0]
            S.dma("sp", qT[:], self.qT_d[hd, :, :], reads=[S.B("qT_d", hd)], writes=[bq_])
            S.dma("sp", kT[:], self.kT_d[hd, :, :], reads=[S.B("kT_d", hd)], writes=[bk_])
            S.dma("sp", Kt[:], self.K_d[:, hd * 128:(hd + 1) * 128].rearrange("(i p) c -> p i c", p=128), reads=[S.B("K_d")], writes=[bK_])
            S.dma("sp", Vt[:, :, 0:256], self.V_d[:, hd * 256:(hd + 1) * 256].rearrange("(i p) c -> p i c", p=128), reads=[S.B("V_d")], writes=[bV_])
            if d == 1:
                S.dma("sp", C32[:], self.xi1[:, hd, :], reads=[S.B("xs1", self.v ^ 1)], writes=[bC])
                S.op("act", lambda e, Cb=Cb, C32=C32: e.activation(Cb[:, 0:258], C32[:], AF.Copy), reads=[bC], writes=[bC])
            first = True
            col = d * 8 + hd
            yield
            for i in tiles:
                p = j
                pS, pO, pU = (0, 2, 4) if p == 0 else (1, 3, 5)
                tsl = slice(i * 128, (i + 1) * 128)
                S.op("pe", lambda e, pS=pS, kT=kT, qT=qT, tsl=tsl: e.matmul(self.bank(pS)[:, 0:128], kT[:, tsl], qT[:, tsl], start=True, stop=True),
                     reads=[bk_, bq_], writes=[self.PB(pS)])
                S.op("dve", lambda e, pS=pS, p=p, i=i, col=col: e.scalar_tensor_tensor(PT[p][:], self.bank(pS)[:, 0:128], gw[:, i, col:col + 1], mask[:], ALU.mult, ALU.mult),
                     reads=[self.PB(pS), bq, self.bconst], writes=[S.B(nm, "PT", p)])
                S.op("pool", lambda e, p=p, i=i, col=col, Kt=Kt: e.tensor_scalar(Kw[p][:], Kt[:, i, :], gs[:, i, col:col + 1], None, ALU.mult),
                     reads=[bK_, bq], writes=[S.B(nm, "Kw", p)])
                nostate = first and d == 0
                S.op("pe", lambda e, pO=pO, p=p, i=i, Vt=Vt, nostate=nostate: e.matmul(self.bank(pO)[:, 0:258], PT[p][:], Vt[:, i, 0:258], start=True, stop=nostate),
                     reads=[S.B(nm, "PT", p), bV_], writes=[self.PB(pO)], inc=nostate)
                if not nostate:
                    S.op("pe", lambda e, pO=pO, qT=qT, tsl=tsl, Cb=Cb: e.matmul(self.bank(pO)[:, 0:258], qT[:, tsl], Cb[:, 0:258], start=False, stop=True),
                         reads=[bq_, bC], writes=[self.PB(pO)])
                S.op("pe", lambda e, pU=pU, p=p, i=i, Vt=Vt: e.matmul(self.bank(pU)[:, 0:258], Kw[p][:], Vt[:, i, 0:258], start=True, stop=True),
                     reads=[S.B(nm, "Kw", p), bV_], writes=[self.PB(pU)])
                bsc = S.B(nm, "sc", j)
                S.op("dve", lambda e, pO=pO, i=i, col=col: e.tensor_tensor(sc[:, 0:1], self.bank(pO)[:, 256:257], gei[:, i, col:col + 1], ALU.max),
                     reads=[self.PB(pO), bq], writes=[bsc])
                S.op("dve", lambda e, pO=pO: e.scalar_tensor_tensor(sc[:, 1:2], self.bank(pO)[:, 256:257], -1.0, sc[:, 0:1], ALU.mult, ALU.max),
                     reads=[self.PB(pO), bsc], writes=[bsc])
                S.op("dve", lambda e: e.reciprocal(sc[:, 2:3], sc[:, 1:2]), reads=[bsc], writes=[bsc])
                ysl = y_acc[:, i, hd * 256:(hd + 1) * 256]
                by = S.B("y_acc", i, hd)
                if d == 0:
                    S.op("act", lambda e, pO=pO, ysl=ysl: e.activation(ysl, self.bank(pO)[:, 0:256], AF.Copy, scale=sc[:, 2:3]),
                         reads=[self.PB(pO), bsc], writes=[by])
                else:
                    btm = S.B(nm, "tmpy", j)
                    S.op("act", lambda e, pO=pO: e.activation(tmpy[j][:], self.bank(pO)[:, 0:256], AF.Copy, scale=sc[:, 2:3]),
                         reads=[self.PB(pO), bsc], writes=[btm])
                    S.op("pool", lambda e, ysl=ysl: e.tensor_tensor(ysl, ysl, tmpy[j][:], ALU.add), reads=[btm, by], writes=[by])
                nxt = 1 - cur
                C32n, Cbn, bCn = C32s[nxt], Cbs[nxt], bCs[nxt]
                if nostate:
                    S.op("dve", lambda e, pU=pU, C32n=C32n: e.tensor_copy(C32n[:], self.bank(pU)[:, 0:258]), reads=[self.PB(pU)], writes=[bCn])
                else:
                    S.op("dve", lambda e, pU=pU, C32=C32, C32n=C32n, i=i, col=col: e.scalar_tensor_tensor(C32n[:], C32[:], gd[:, i, col:col + 1], self.bank(pU)[:, 0:258], ALU.mult, ALU.add),
                         reads=[self.PB(pU), bq, bC], writes=[bCn])
                S.op("act", lambda e, Cbn=Cbn, C32n=C32n: e.activation(Cbn[:, 0:258], C32n[:], AF.Copy), reads=[bCn], writes=[bCn])
                cur = nxt
                C32, Cb, bC = C32n, Cbn, bCn
                first = False
                yield
            if d == 0:
                S.dma("sp", self.xo1[:, hd, :], C32[:], reads=[bC], writes=[S.B("xs1", self.v)])

        for hd in range(0, A_H, 2):
            rr([head_gen(hd), head_gen(hd + 1)])

    def head_norm(self, st, y_acc, yT, bT, nh, head_g_row, gate_d, bgate_d):
        S = self.S
        dh = D // nh
        self.load_gbc(head_g_row)
        with ExitStack() as s2:
            sq = self.sb(s2, "hn_sq", [128, D], F32)
            gt = [self.sb(s2, "hn_gt%d" % j, [128, D], BF16) for j in range(2)]
            gso = self.sb(s2, "hn_gso", [128, D], BF16)
            yb = [self.sb(s2, "hn_yb%d" % j, [128, D], BF16) for j in range(2)]
            ssh = self.sb(s2, "hn_ss", [128, 3, 16], F32)
            for i in range(NT):
                j = i % 2
                bys = [S.B("y_acc", i, hd) for hd in range(nh)]
                S.dma("sp", gt[j][:], gate_d[i * 128:(i + 1) * 128, :], reads=[bgate_d], writes=[S.B("hn_gt", j)])
                S.op("pool", lambda e, i=i: e.tensor_tensor(sq[:], y_acc[:, i, :], y_acc[:, i, :], ALU.mult), reads=bys, writes=[S.B("hn_sq")])
                S.op("dve", lambda e: e.tensor_reduce(ssh[:, 0, 0:nh], sq[:].rearrange("p (h d) -> p h d", h=nh), AX.X, ALU.add), reads=[S.B("hn_sq")], writes=[S.B("hn_ss")])
                S.op("act", lambda e: e.activation(ssh[:, 1, 0:nh], ssh[:, 0, 0:nh], AF.Ln, bias=EPS, scale=1.0 / dh), reads=[S.B("hn_ss")], writes=[S.B("hn_ss")])
                S.op("act", lambda e: e.activation(ssh[:, 2, 0:nh], ssh[:, 1, 0:nh], AF.Exp, scale=-0.5), reads=[S.B("hn_ss")], writes=[S.B("hn_ss")])
                S.op("pool", lambda e, j=j: e.tensor_tensor(gso[:], gt[j][:], self.gbc[:], ALU.mult), reads=[S.B("hn_gt", j), S.B("gbc")], writes=[S.B("hn_gso")])
                S.op("dve", lambda e, i=i: e.tensor_tensor(sq[:].rearrange("p (h d) -> p h d", h=nh), y_acc[:, i, :].rearrange("p (h d) -> p h d", h=nh),
                                                          ssh[:, 2, 0:nh].unsqueeze(2).to_broadcast([128, nh, dh]), ALU.mult),
                     reads=bys + [S.B("hn_ss")], writes=[S.B("hn_sq")])
                S.op("dve", lambda e, j=j: e.tensor_tensor(yb[j][:], sq[:], gso[:], ALU.mult), reads=[S.B("hn_sq"), S.B("hn_gso")], writes=[S.B("hn_yb", j)])
                self.transpose_tile(yb[j], S.B("hn_yb", j), yT, bT, i)

    def out_proj(self, st, yT, bT, w_ap):
        S = self.S
        OB = 256
        with ExitStack() as s2:
            slots = [self.sb(s2, "owslot%d" % j, [128, KC, OB], BF16) for j in range(3)]
            self.slot_rr = 0
            n = 0
            for blk in range(D // OB):
                slot, bslot = self.load_w(slots, w_ap, blk * OB, OB)
                for i in range(NT):
                    bj = n % 4
                    n += 1
                    self.proj_tm(yT, bT, slot, bslot, i, bj, ncols=OB)
                    hs = self.h[:, i, blk * OB:(blk + 1) * OB]
                    S.op("dve", lambda e, hs=hs, bj=bj: e.tensor_tensor(hs, hs, self.bank(bj)[:, 0:OB], ALU.add),
                         reads=[self.PB(bj), S.B("h", i)], writes=[S.B("h", i)])

    def ffn_layer(self, l, part):
        S = self.S
        xo = self.xo2 if l == 0 else self.xo4
        xi = self.xi2 if l == 0 else self.xi4
        xkey = "xs2" if l == 0 else "xs4"
        with ExitStack() as st:
            hnT = self.sb(st, "fhnT%d" % l, [128, KC, T + 2], BF16)
            bT = S.B("fhnT")
            self.norm_to_T(st, hnT, bT, self.norm_ffn_g[l:l + 1, :], tiles=([NT - 1] if part == "halo" else None))
            hal = self.sb(st, "halo%d" % l, [128, KC], F32)
            if part == "halo":
                S.op("dve", lambda e: e.tensor_copy(hal[:], hnT[:, :, T - 1]), reads=[bT], writes=[S.B("halo")])
                S.dma("sp", xo[:, :], hal[:], reads=[S.B("halo")], writes=[S.B(xkey, self.v)])
                S.barrier()
                return
            hal2 = self.sb(st, "halo2_%d" % l, [128, KC], F32)
            if self.dyn:
                xs4_all = self.xs4_all
                S.dma("sp", None, None, reads=[S.B(xkey, 0), S.B(xkey, 1)], writes=[S.B("halo2")],
                      fn=lambda e: e.dma_start(out=hal2[:], in_=xs4_all[bass.ds((S.pid + 1) % 2, 1), :, :]))
            else:
                S.dma("sp", hal2[:], xi[:, :], reads=[S.B(xkey, self.v ^ 1)], writes=[S.B("halo2")])
            S.op("dve", lambda e: e.tensor_copy(hnT[:, :, T], hal2[:]), reads=[S.B("halo2"), bT], writes=[bT])
            cw = self.sb(st, "cw%d" % l, [128, 4, 64], F32)
            bcw = []
            for j in range(3):
                jj = j if (self.v == 0 or self.dyn) else 2 - j
                _, b = self.load_fm_vec(st, "cw%d_%d" % (l, j), getattr(self, "f_conv_w%d" % l)[jj:jj + 1, :], NFF, dst=cw[:, j, :])
                bcw.append(b)
            _, b = self.load_fm_vec(st, "cb%d" % l, getattr(self, "f_conv_b%d" % l)[0:1, :], NFF, dst=cw[:, 3, :])
            bcw.append(b)
            GR = 3
            slots = [self.sb(st, "fwslot%d_%d" % (l, j), [128, KC, GR * 128], BF16) for j in range(3)]
            self.slot_rr = 0
            wd = [self.sb(st, "wd%d_%d" % (l, j), [128, D], BF16) for j in range(2 * GR)]
            gT = [self.sb(st, "gT%d_%d" % (l, j), [128, T], BF16) for j in range(2 * GR)]
            csb = self.sb(st, "csb%d" % l, [128, T], F32)
            gel = self.sb(st, "gel%d" % l, [128, T], BF16)
            w_up = getattr(self, "f_w_up%d" % l)
            w_down = getattr(self, "f_w_down%d" % l)
            ngroups = (NFF + GR - 1) // GR
            pa = self.ps[0]
            pv = self.ps[1]
            ndp = 0
            for g in range(ngroups):
                c_lo = g * GR
                nch = min(GR, NFF - c_lo)
                ncol = nch * 128
                sa, bsa = self.load_w(slots, w_up, c_lo * 128, ncol)
                sv, bsv = self.load_w(slots, w_up, DFF + c_lo * 128, ncol)
                ring = (g % 2) * GR
                for mc in range(nch):
                    c = c_lo + mc
                    S.dma("pool", wd[ring + mc][:], w_down[c * 128:(c + 1) * 128, :], writes=[S.B("wd", l, ring + mc)])
                for mc in range(nch):
                    c = c_lo + mc
                    for kc in range(KC):
                        lhs = sa[:, kc, mc * 128:(mc + 1) * 128]
                        S.op("pe", lambda e, lhs=lhs, kc=kc: e.matmul(pa[:, 0:512], lhs, hnT[:, kc, 0:512], start=(kc == 0), stop=(kc == KC - 1)),
                             reads=[bT, bsa[kc // 4]], writes=[self.PB(0)], inc=False)
                        S.op("pe", lambda e, lhs=lhs, kc=kc: e.matmul(pa[:, 512:1024], lhs, hnT[:, kc, 512:1024], start=(kc == 0), stop=(kc == KC - 1)),
                             reads=[bT, bsa[kc // 4]], writes=[self.PB(1)], inc=False)
                        S.op("pe", lambda e, lhs=lhs, kc=kc: e.matmul(self.bank(4)[:, 0:1], lhs, hnT[:, kc, T:T + 1], start=(kc == 0), stop=(kc == KC - 1)),
                             reads=[bT, bsa[kc // 4]], writes=[self.PB(4)], inc=(kc == KC - 1))
                    for kc in range(KC):
                        lhs = sv[:, kc, mc * 128:(mc + 1) * 128]
                        S.op("pe", lambda e, lhs=lhs, kc=kc: e.matmul(pv[:, 0:512], lhs, hnT[:, kc, 0:512], start=(kc == 0), stop=(kc == KC - 1)),
                             reads=[bT, bsv[kc // 4]], writes=[self.PB(2)], inc=False)
                        S.op("pe", lambda e, lhs=lhs, kc=kc: e.matmul(pv[:, 512:1024], lhs, hnT[:, kc, 512:1024], start=(kc == 0), stop=(kc == KC - 1)),
                             reads=[bT, bsv[kc // 4]], writes=[self.PB(3)], inc=(kc == KC - 1))
                    bcs = S.B("csb")
                    pab = [self.PB(0), self.PB(1)]
                    S.op("dve", lambda e, c=c: e.tensor_scalar(csb[:], pa[:, 0:T], cw[:, 1, c:c + 1], cw[:, 3, c:c + 1], ALU.mult, ALU.add),
                         reads=pab + bcw, writes=[bcs])
                    S.op("dve", lambda e, c=c: e.scalar_tensor_tensor(csb[:, 1:T], pa[:, 0:T - 1], cw[:, 0, c:c + 1], csb[:, 1:T], ALU.mult, ALU.add),
                         reads=pab + bcw + [bcs], writes=[bcs])
                    S.op("dve", lambda e, c=c: e.scalar_tensor_tensor(csb[:, 0:T - 1], pa[:, 1:T], cw[:, 2, c:c + 1], csb[:, 0:T - 1], ALU.mult, ALU.add),
                         reads=pab + bcw + [bcs], writes=[bcs])
                    S.op("dve", lambda e, c=c: e.scalar_tensor_tensor(csb[:, T - 1:T], self.bank(4)[:, 0:1], cw[:, 2, c:c + 1], csb[:, T - 1:T], ALU.mult, ALU.add),
                         reads=[self.PB(4)] + bcw + [bcs], writes=[bcs])
                    S.op("act", lambda e: e.activation(gel[:], csb[:], AF.Gelu), reads=[bcs], writes=[S.B("gel")])
                    S.op("dve", lambda e, mc=mc, ring=ring: e.tensor_tensor(gT[ring + mc][:], gel[:], pv[:, 0:T], ALU.mult),
                         reads=[S.B("gel"), self.PB(2), self.PB(3)], writes=[S.B("gT", l, ring + mc)])
                for i in range(NT):
                    for nb in range(4):
                        bj = 5 + (ndp % 3)
                        ndp += 1
                        for mc in range(nch):
                            S.op("pe", lambda e, mc=mc, ring=ring, i=i, nb=nb, bj=bj, nch=nch: e.matmul(self.bank(bj)[:, :], gT[ring + mc][:, i * 128:(i + 1) * 128], wd[ring + mc][:, nb * 512:(nb + 1) * 512], start=(mc == 0), stop=(mc == nch - 1)),
                                 reads=[S.B("gT", l, ring + mc), S.B("wd", l, ring + mc)], writes=[self.PB(bj)], inc=(mc == nch - 1))
                        hs = self.h[:, i, nb * 512:(nb + 1) * 512]
                        S.op("dve", lambda e, hs=hs, bj=bj: e.tensor_tensor(hs, hs, self.bank(bj)[:, :], ALU.add),
                             reads=[self.PB(bj), S.B("h", i)], writes=[S.B("h", i)])
            S.barrier()

    def hgrn_layer(self, part):
        S = self.S
        if part in ("a", "ab"):
            self.hgrn_inproj()
        with ExitStack() as st:
            y_acc = self.sb(st, "y2_acc", [128, NT, D], F32)
            if part in ("a", "ab"):
                with ExitStack() as st2:
                    self.hgrn_scan(st2, y_acc, 0)
                    if part == "a":
                        self.move_yacc(y_acc, B_H, store=True)
                    S.barrier()
                if part == "a":
                    return
            if part == "b":
                self.move_yacc(y_acc, B_H, store=False)
            with ExitStack() as st2:
                self.hgrn_scan(st2, y_acc, 1)
                S.barrier()
            yT = self.sb(st, "hyT", [128, KC, T + 2], BF16)
            bT = S.B("hyT")
            self.head_norm(st, y_acc, yT, bT, B_H, self.h_head_g, self.O_d, S.B("O_d"))
            S.barrier()
            self.out_proj(st, yT, bT, self.h_w_out)
            S.barrier()

    def hgrn_inproj(self):
        S = self.S
        lbv = self.lbv
        blb = S.B("lbv")
        with ExitStack() as st:
            hnT = self.sb(st, "hhnT", [128, KC, T + 2], BF16)
            bT = S.B("hhnT")
            self.norm_to_T(st, hnT, bT, self.norm_mix_g[1:2, :])
            l0, b0 = self.load_fm_vec(st, "hlb0", self.h_lb[0:1, :], 16)
            l1, b1 = self.load_fm_vec(st, "hlb1", self.h_lb[1:2, :], 16)
            S.op("dve", lambda e: e.tensor_tensor(lbv[:, 3, :], l1[:, 0:16], l0[:, 0:16], ALU.subtract), reads=[b0, b1], writes=[blb])
            S.op("act", lambda e: e.activation(lbv[:, 0, :], lbv[:, 3, :], AF.Sigmoid), reads=[blb], writes=[blb])
            S.op("dve", lambda e: e.tensor_scalar(lbv[:, 2, :], lbv[:, 0, :], -1.0, None, ALU.add), reads=[blb], writes=[blb])
            S.op("dve", lambda e: e.tensor_scalar(lbv[:, 1, :], lbv[:, 2, :], -1.0, None, ALU.mult), reads=[blb], writes=[blb])
            slots = [self.sb(st, "hwslot%d" % j, [128, KC, WB], BF16) for j in range(3)]
            self.slot_rr = 0
            stg_q = [self.sb(st, "hstgq%d" % j, [128, T], BF16) for j in range(2)]
            stg_f = [self.sb(st, "hstgf%d" % j, [128, T], F32) for j in range(2)]
            stg_tm = [self.sb(st, "hstgtm%d" % j, [128, NT, WB], BF16) for j in range(2)]
            n_fm = 0
            fb1, fb2 = (12, 16) if self.v == 0 else (16, 12)
            for kind, blk0, dst in (("q", 0, self.qT_d), ("f1", fb1, self.f1T_d), ("f2", fb2, self.f2T_d)):
                for b4 in range(4):
                    slot, bslot = self.load_w(slots, self.h_w_in, (blk0 + b4) * WB, WB)
                    for mc in range(4):
                        head = b4 * 4 + mc
                        banks = (0, 1) if (n_fm % 2 == 0) else (2, 3)
                        sj = n_fm % 2
                        n_fm += 1
                        self.proj_fm(hnT, bT, slot, bslot, mc, banks)
                        stg = stg_q[sj] if kind == "q" else stg_f[sj]
                        bst = S.B("hstg", kind == "q", sj)
                        for hf in range(2):
                            if hf == 0:
                                S.op("act", lambda e, hf=hf, stg=stg, banks=banks: e.activation(stg[:, hf * 512:(hf + 1) * 512], self.bank(banks[hf])[:, :], AF.Copy),
                                     reads=[self.PB(banks[hf])], writes=[bst])
                            else:
                                S.op("dve", lambda e, hf=hf, stg=stg, banks=banks: e.tensor_copy(stg[:, hf * 512:(hf + 1) * 512], self.bank(banks[hf])[:, :]),
                                     reads=[self.PB(banks[hf])], writes=[bst])
                        S.dma("sp", dst[head, :, :], stg[:], reads=[bst], writes=[S.B("h_" + kind + "_d", head)])
            n_tm = 0
            for blk in range(4, 12):
                slot, bslot = self.load_w(slots, self.h_w_in, blk * WB, WB)
                sj = blk % 2
                kind = "v" if blk < 8 else "g"
                for i in range(NT):
                    bj = 4 + (n_tm % 2)
                    n_tm += 1
                    self.proj_tm(hnT, bT, slot, bslot, i, bj)
                    if kind == "g":
                        S.op("act", lambda e, i=i, sj=sj, bj=bj: e.activation(stg_tm[sj][:, i, :], self.bank(bj)[:, :], AF.Silu),
                             reads=[self.PB(bj)], writes=[S.B("hstgtm", sj)])
                    else:
                        S.op("dve", lambda e, i=i, sj=sj, bj=bj: e.tensor_copy(stg_tm[sj][:, i, :], self.bank(bj)[:, :]),
                             reads=[self.PB(bj)], writes=[S.B("hstgtm", sj)])
                if kind == "v":
                    dst, c0, bd = self.V_d, (blk - 4) * WB, S.B("V_d")
                else:
                    dst, c0, bd = self.O_d, (blk - 8) * WB, S.B("O_d")
                S.dma("sp", dst[:, c0:c0 + WB].rearrange("(i p) c -> p i c", p=128), stg_tm[sj][:], reads=[S.B("hstgtm", sj)], writes=[bd])
            S.barrier()

    def hgrn_scan(self, st, y_acc, d):
        S = self.S
        lbv = self.lbv
        blb = S.B("lbv")
        nm = "hs%d" % d
        tiles = list(range(NT)) if d == 0 else list(range(NT - 1, -1, -1))
        mask = self.m_le if d == 0 else self.m_ge
        fT_d = self.f1T_d if d == 0 else self.f2T_d
        qT_d, V_d, xi3, xo3 = self.qT_d, self.V_d, self.xi3, self.xo3
        vv = self.v
        fkind = "f1" if d == 0 else "f2"
        ins = []
        qh1 = self.sb(st, nm + "qh", [128, T], BF16)
        fh1 = self.sb(st, nm + "fh", [128, T], F32)
        for j in range(2):
            Vh = self.sb(st, nm + "Vh%d" % j, [128, NT, 128], BF16)
            ins.append((qh1, fh1, Vh))
        t_sig, t_f, t_P, t_k = [self.sb(st, nm + "t%d" % k, [128, T], F32) for k in range(4)]
        b_sig, b_f, b_P, b_k = [S.B(nm, "t", k) for k in range(4)]
        rst = self.sb(st, nm + "rst", [128, T], BF16)
        prods2, Kt2, dec2 = [], [], []
        for jj in range(2):
            prods = [self.sb(st, nm + "pr%d_%d" % (k, jj), [128, T], BF16) for k in range(6)]
            for pr in prods:
                S.op("pool", lambda e, pr=pr: e.memset(pr[:], 0.0), writes=[S.B(nm, "prods", jj)])
            prods2.append(prods)
            Kt2.append((self.sb(st, nm + "KAt%d" % jj, [128, NT, 128], BF16), self.sb(st, nm + "KBt%d" % jj, [128, NT, 128], BF16)))
            dec2.append(self.sb(st, nm + "dec%d" % jj, [128, 16], F32))
        PT = [self.sb(st, nm + "PT%d" % j, [128, 128], BF16) for j in range(2)]
        S32 = self.sb(st, nm + "S32", [128, 128], F32)
        S32b = self.sb(st, nm + "S32b", [128, 128], F32)
        Sb = [self.sb(st, nm + "Sb%d" % j, [128, 128], BF16) for j in range(2)]
        S.op("pool", lambda e: e.memset(rst[:], 1.0), writes=[S.B(nm, "rst")])
        S.op("pool", lambda e: e.memset(rst[:].rearrange("p (c l) -> p c l", l=64)[:, :, 0:1], 0.0), writes=[S.B(nm, "rst")])

        def ev(x):
            return x[:].rearrange("p (c two l) -> p c two l", two=2, l=64)[:, :, 0, :]

        def od(x):
            return x[:].rearrange("p (c two l) -> p c two l", two=2, l=64)[:, :, 1, :]

        def c3(x):
            return x[:].rearrange("p (c l) -> p c l", l=64)

        def gm(m):
            j = m % 2
            qh, fh, Vh = ins[j]
            qA, qB, kA, kB, xA, xB = prods2[j]
            KAt, KBt = Kt2[j]
            dec = dec2[j]
            bpr, bdec, bKt = S.B(nm, "prods", j), S.B(nm, "dec", j), S.B(nm, "Kt", j)
            bq_, bf_, bV_ = S.B(nm, "qh"), S.B(nm, "fh"), S.B(nm, "Vh", j)
            S.dma("sp", qh[:], qT_d[m, :, :], reads=[S.B("h_q_d", m)], writes=[bq_])
            S.dma("sp", fh[:], fT_d[m, :, :], reads=[S.B("h_" + fkind + "_d", m)], writes=[bf_])
            S.dma("sp", Vh[:], V_d[:, m * 128:(m + 1) * 128].rearrange("(i p) c -> p i c", p=128), reads=[S.B("V_d")], writes=[bV_])
            yield
            S.op("act", lambda e: e.activation(t_sig[:], fh[:], AF.Sigmoid), reads=[bf_], writes=[b_sig])
            yield
            S.op("dve", lambda e: e.tensor_scalar(t_f[:], t_sig[:], lbv[:, 1, m:m + 1], lbv[:, 0, m:m + 1], ALU.mult, ALU.add), reads=[b_sig, blb], writes=[b_f])
            yield
            S.op("pool", lambda e: e.tensor_scalar(t_k[:], t_sig[:], lbv[:, 2, m:m + 1], lbv[:, 1, m:m + 1], ALU.mult, ALU.add), reads=[b_sig, blb], writes=[b_k])
            yield
            S.op("act", lambda e: e.activation(t_f[:], t_f[:], AF.Ln), reads=[b_f], writes=[b_f])
            yield
            S.op("dve", lambda e: e.tensor_tensor_scan(t_P[:], rst[:], t_f[:], 0.0, ALU.mult, ALU.add), reads=[b_f, S.B(nm, "rst")], writes=[b_P])
            yield
            S.op("act", lambda e: e.activation(dec[:], c3(t_P)[:, :, 63], AF.Exp), reads=[b_P], writes=[bdec])
            yield
            if d == 0:
                E, bE = t_P, b_P
                sq, sk = 1.0, -1.0
            else:
                S.op("pool", lambda e: e.tensor_tensor(t_sig[:], t_P[:], t_f[:], ALU.subtract), reads=[b_P, b_f], writes=[b_sig])
                yield
                E, bE = t_sig, b_sig
                sq, sk = -1.0, 1.0
            t_e1, b_e1 = t_f, b_f
            t_e2, b_e2 = E, bE
            S.op("act", lambda e: e.activation(t_e1[:], E[:], AF.Exp, scale=sq), reads=[bE], writes=[b_e1])
            yield
            S.op("act", lambda e: e.activation(t_e2[:], E[:], AF.Exp, scale=sk), reads=[bE], writes=[b_e2])
            yield
            S.op("dve", lambda e: e.tensor_tensor(ev(qA), ev(qh), ev(t_e1), ALU.mult), reads=[bq_, b_e1], writes=[bpr])
            yield
            S.op("dve", lambda e: e.tensor_tensor(od(qB), od(qh), od(t_e1), ALU.mult), reads=[bq_, b_e1], writes=[bpr])
            yield
            S.op("pool", lambda e: e.tensor_tensor(ev(kA), ev(t_k), ev(t_e2), ALU.mult), reads=[b_k, b_e2], writes=[bpr])
            yield
            S.op("pool", lambda e: e.tensor_tensor(od(kB), od(t_k), od(t_e2), ALU.mult), reads=[b_k, b_e2], writes=[bpr])
            yield
            srcA, srcB = (kA, kB) if d == 0 else (qA, qB)
            decA = dec[:, :].rearrange("p (c two) -> p c two", two=2)[:, :, 0:1].to_broadcast([128, 8, 64])
            decB = dec[:, :].rearrange("p (c two) -> p c two", two=2)[:, :, 1:2].to_broadcast([128, 8, 64])
            S.op("pool", lambda e: e.tensor_tensor(ev(xA), ev(srcA), decA, ALU.mult), reads=[bpr, bdec], writes=[bpr])
            yield
            S.op("pool", lambda e: e.tensor_tensor(od(xB), od(srcB), decB, ALU.mult), reads=[bpr, bdec], writes=[bpr])
            yield
            tA, tB = (xA, xB) if d == 0 else (kA, kB)
            for (src, dstT, pj) in ((tA, KAt, 6), (tB, KBt, 7)):
                pvb = self.bank(pj).bitcast(BF16)
                for i in range(NT):
                    S.op("pe", lambda e, src=src, i=i, pvb=pvb: e.transpose(pvb[:, i * 128:(i + 1) * 128], src[:, i * 128:(i + 1) * 128], self.ident[:]),
                         reads=[bpr, self.bconst], writes=[self.PB(pj)], inc=(i == NT - 1))
                S.op("act", lambda e, dstT=dstT, pvb=pvb: e.activation(dstT[:], pvb[:, :].rearrange("p (a b) -> p a b", a=NT), AF.Copy),
                     reads=[self.PB(pj)], writes=[bKt])
                yield

        def tl(m):
            j = m % 2
            qh, fh, Vh = ins[j]
            qA, qB, kA, kB, xA, xB = prods2[j]
            KAt, KBt = Kt2[j]
            dec = dec2[j]
            bpr, bdec, bKt = S.B(nm, "prods", j), S.B(nm, "dec", j), S.B(nm, "Kt", j)
            bV_ = S.B(nm, "Vh", j)
            bS = S.B(nm, "S")
            QA, QB = (qA, qB) if d == 0 else (xA, xB)
            if d == 0:
                S.op("pool", lambda e: e.memset(S32[:], 0.0), writes=[bS])
            else:
                S.dma("sp", S32[:], xi3[:, m, :], reads=[S.B("xs3", vv ^ 1)], writes=[bS])
            S.op("act", lambda e: e.activation(Sb[0][:], S32[:], AF.Copy), reads=[bS], writes=[S.B(nm, "Sb", 0)])
            yield
            for n_it, i in enumerate(tiles):
                p = n_it % 2
                pA, pO = (0, 2) if p == 0 else (1, 3)
                pUx, pUy = 4, 5
                tsl = slice(i * 128, (i + 1) * 128)
                cX, cY = (2 * i, 2 * i + 1) if d == 0 else (2 * i + 1, 2 * i)
                KX, KY = (KAt, KBt) if d == 0 else (KBt, KAt)
                QX, QY = (QA, QB) if d == 0 else (QB, QA)
                S.op("pe", lambda e, pA=pA, tsl=tsl: e.matmul(self.bank(pA)[:, 0:128], kA[:, tsl], qA[:, tsl], start=True, stop=False),
                     reads=[bpr], writes=[self.PB(pA)], inc=False)
                S.op("pe", lambda e, pA=pA, tsl=tsl: e.matmul(self.bank(pA)[:, 0:128], kB[:, tsl], qB[:, tsl], start=False, stop=True),
                     reads=[bpr], writes=[self.PB(pA)])
                S.op("dve", lambda e, pA=pA, p=p: e.tensor_tensor(PT[p][:], self.bank(pA)[:, 0:128], mask[:], ALU.mult),
                     reads=[self.PB(pA), self.bconst], writes=[S.B(nm, "PT", p)])
                S.op("pe", lambda e, i=i, KX=KX: e.matmul(self.bank(pUx)[:, 0:128], KX[:, i, :], Vh[:, i, :], start=True, stop=True),
                     reads=[bKt, bV_], writes=[self.PB(pUx)])
                S.op("pe", lambda e, i=i, KY=KY: e.matmul(self.bank(pUy)[:, 0:128], KY[:, i, :], Vh[:, i, :], start=True, stop=True),
                     reads=[bKt, bV_], writes=[self.PB(pUy)])
                S.op("pe", lambda e, pO=pO, p=p, i=i: e.matmul(self.bank(pO)[:, 0:128], PT[p][:], Vh[:, i, :], start=True, stop=False),
                     reads=[S.B(nm, "PT", p), bV_], writes=[self.PB(pO)], inc=False)
                S.op("pe", lambda e, pO=pO, QX=QX, tsl=tsl: e.matmul(self.bank(pO)[:, 0:128], QX[:, tsl], Sb[0][:], start=False, stop=False),
                     reads=[bpr, S.B(nm, "Sb", 0)], writes=[self.PB(pO)], inc=False)
                bS2 = S.B(nm, "S2")
                S.op("dve", lambda e, cX=cX: e.scalar_tensor_tensor(S32b[:], S32[:], dec[:, cX:cX + 1], self.bank(pUx)[:, 0:128], ALU.mult, ALU.add),
                     reads=[self.PB(pUx), bdec, bS], writes=[bS2])
                S.op("act", lambda e: e.activation(Sb[1][:], S32b[:], AF.Copy), reads=[bS2], writes=[S.B(nm, "Sb", 1)])
                S.op("pe", lambda e, pO=pO, QY=QY, tsl=tsl: e.matmul(self.bank(pO)[:, 0:128], QY[:, tsl], Sb[1][:], start=False, stop=True),
                     reads=[bpr, S.B(nm, "Sb", 1)], writes=[self.PB(pO)])
                S.op("dve", lambda e, cY=cY: e.scalar_tensor_tensor(S32[:], S32b[:], dec[:, cY:cY + 1], self.bank(pUy)[:, 0:128], ALU.mult, ALU.add),
                     reads=[self.PB(pUy), bdec, bS2], writes=[bS])
                S.op("act", lambda e: e.activation(Sb[0][:], S32[:], AF.Copy), reads=[bS], writes=[S.B(nm, "Sb", 0)])
                ysl = y_acc[:, i, m * 128:(m + 1) * 128]
                by = S.B("y_acc", i, m)
                if d == 0:
                    S.op("act", lambda e, pO=pO, ysl=ysl: e.activation(ysl, self.bank(pO)[:, 0:128], AF.Copy), reads=[self.PB(pO)], writes=[by])
                else:
                    S.op("dve", lambda e, pO=pO, ysl=ysl: e.tensor_tensor(ysl, ysl, self.bank(pO)[:, 0:128], ALU.add), reads=[self.PB(pO), by], writes=[by])
                yield
            if d == 0:
                S.dma("sp", xo3[:, m, :], S32[:], reads=[bS], writes=[S.B("xs3", vv)])

        def drain(g):
            for _ in g:
                pass

        drain(gm(0))
        for m in range(B_H):
            a = tl(m)
            b = gm(m + 1) if m + 1 < B_H else iter(())
            done_a = done_b = False
            while not (done_a and done_b):
                if not done_a:
                    try:
                        next(a)
                    except StopIteration:
                        done_a = True
                for _ in range(2):
                    if not done_b:
                        try:
                            next(b)
                        except StopIteration:
                            done_b = True

    def final_norm(self):
        S = self.S
        self.load_gbc(self.final_g[0:1, :])
        with ExitStack() as st:
            ob = [self.sb(st, "fo%d" % j, [128, D], F32) for j in range(2)]
            for i in range(NT):
                j = i % 2
                bss = S.B("ss")
                S.op("act", lambda e, j=j, i=i: e.activation(ob[j][:], self.h[:, i, :], AF.Square, accum_out=self.ss[:, i:i + 1]), reads=[S.B("h", i)], writes=[S.B("fo", j), bss])
                S.op("act", lambda e, i=i: e.activation(self.ss[:, 32 + i:33 + i], self.ss[:, i:i + 1], AF.Ln, bias=EPS, scale=1.0 / D), reads=[bss], writes=[bss])
                S.op("act", lambda e, i=i: e.activation(self.ss[:, 48 + i:49 + i], self.ss[:, 32 + i:33 + i], AF.Exp, scale=-0.5), reads=[bss], writes=[bss])
                S.op("dve", lambda e, j=j, i=i: e.scalar_tensor_tensor(ob[j][:], self.h[:, i, :], self.ss[:, 48 + i:49 + i], self.gbc[:], ALU.mult, ALU.mult), reads=[S.B("h", i), bss, S.B("gbc")], writes=[S.B("fo", j)])
                S.dma("sp", self.y[i * 128:(i + 1) * 128, :], ob[j][:], reads=[S.B("fo", j)], writes=[S.B("y")])


def _fm(w):
    n = w.shape[1]
    return np.ascontiguousarray(w.reshape(KC, 128, n).transpose(1, 0, 2).reshape(128, KC * n))


def _prep(inputs, n_cores=8):
    f = lambda k: np.ascontiguousarray(np.asarray(inputs[k], dtype=np.float32))
    x = f("x")
    w_in = f("mlstm_w_in")[0]
    bg = f("mlstm_b_gate")[0]
    perm_g = np.concatenate([np.arange(16, 32), np.arange(0, 16)])
    wg = w_in[:, 6144:6176]
    common = {
        "norm_mix_g": f("norm_mix_g"), "norm_ffn_g": f("norm_ffn_g"),
        "m_w_in": np.ascontiguousarray(w_in[:, :6144]),
        "m_w_gate": np.stack([_fm(wg), _fm(wg[:, perm_g])]),
        "m_b_gate": np.stack([bg, bg[perm_g]]),
        "m_head_g": f("mlstm_head_g").reshape(1, D), "m_w_out": f("mlstm_w_out")[0],
        "h_w_in": f("hgrn_w_in")[0],
        "h_lb": f("hgrn_lb"), "h_head_g": f("hgrn_head_g").reshape(1, D), "h_w_out": f("hgrn_w_out")[0],
        "f_w_up0": f("ffn_w_up")[0], "f_w_up1": f("ffn_w_up")[1],
        "f_conv_w0": f("ffn_conv_w")[0], "f_conv_w1": f("ffn_conv_w")[1],
        "f_conv_b0": f("ffn_conv_b")[0:1], "f_conv_b1": f("ffn_conv_b")[1:2],
        "f_w_down0": f("ffn_w_down")[0], "f_w_down1": f("ffn_w_down")[1],
        "final_g": f("final_g").reshape(1, D),
    }
    nb = x.shape[0]
    maps = []
    cw1 = common["f_conv_w1"]
    for c in range(n_cores):
        b = (c // 2) % nb
        m = dict(common)
        if c % 2 == 1:
            m["f_conv_w1"] = np.ascontiguousarray(cw1[::-1])
        m["x"] = np.ascontiguousarray(np.stack([x[b, :T], x[b, T:][::-1]]))
        maps.append(m)
    return maps


def _run(maps, dbg=None, core_ids=None, stop=None):
    if core_ids is None:
        core_ids = list(range(len(maps)))
    prog = Prog(dbg=dbg, stop=stop, n_cores=len(core_ids))
    nc = prog.build()
    in_maps = [{name: maps[c][name] for name in prog.in_names} for c in core_ids]
    res = run_bass_kernel_spmd(nc, in_maps, core_ids=list(range(len(core_ids))))
    return {c: res.results[j] for j, c in enumerate(core_ids)}


def kernel(**inputs):
    maps = _prep(inputs)
    nb = np.asarray(inputs["x"]).shape[0]
    out = _run(maps)
    y = np.empty((nb, 2 * T, D), np.float32)
    for b in range(nb):
        y[b, :T] = out[2 * b]["y"]
        y[b, T:] = out[2 * b + 1]["y"][::-1]
    return y
```
